# Optimizing a Trainium2 kernel written in Bass

```python
import math
import jax, jax.numpy as jnp
from jax import lax
import numpy as np

D_MODEL = 1024
BATCH = 8
SEQ = 8192
DEPTH = 1

CHUNK = 64
Q_BLOCK = 128
D_MIX = D_MODEL
ATT_WIDTH = D_MIX // 2
POOL_WIDTH = D_MIX - ATT_WIDTH
N_DIFF_HEADS = 4
DIFF_HEAD_DIM = ATT_WIDTH // N_DIFF_HEADS // 2
DIFF_V_DIM = 2 * DIFF_HEAD_DIM
POOL_WINDOWS = (2, 4, 8, 16)
N_POOL_GROUPS = len(POOL_WINDOWS)
POOL_GROUP_DIM = POOL_WIDTH // N_POOL_GROUPS
D_FF = ((8 * D_MODEL // 3 + 255) // 256) * 256
ROPE_THETA = 10000.0
NORM_EPS = 1e-6
LAMBDA_INIT_SCALE = 0.1
IN_WIDTH = 3 * ATT_WIDTH + POOL_WIDTH

kernel_name = "hybrid_diffattn_pool_macaron"


def rms_norm(x, g):
    xf = x.astype(jnp.float32)
    y = xf * lax.rsqrt(jnp.mean(xf * xf, axis=-1, keepdims=True) + NORM_EPS)
    return (y * g.astype(jnp.float32)).astype(x.dtype)


def swiglu(h, w_gate, w_up, w_down):
    return (jax.nn.silu(h @ w_gate) * (h @ w_up)) @ w_down


def rope_tables(seq, dim):
    pos = jnp.arange(seq, dtype=jnp.float32)
    inv_freq = 1.0 / (ROPE_THETA ** (jnp.arange(0, dim, 2, dtype=jnp.float32) / dim))
    ang = pos[:, None] * inv_freq[None, :]
    return jnp.cos(ang), jnp.sin(ang)


def apply_rope(t, cos, sin):
    tf = t.astype(jnp.float32)
    t1, t2 = jnp.split(tf, 2, axis=-1)
    c = cos[:, None, None, :]
    s = sin[:, None, None, :]
    out = jnp.concatenate([t1 * c - t2 * s, t1 * s + t2 * c], axis=-1)
    return out.astype(t.dtype)


def diff_attention(q, k, v, lam):
    B, S, H, _, Dh = q.shape
    nb = S // Q_BLOCK
    scale = DIFF_HEAD_DIM ** -0.5
    qb = (q * scale).reshape(B, nb, Q_BLOCK, H, 2, Dh).transpose(1, 4, 0, 3, 2, 5)
    kt = k.transpose(3, 0, 2, 1, 4)
    vt = v.transpose(0, 2, 1, 3)
    key_chunk = jnp.arange(S) // CHUNK

    def block(args):
        q_blk, bi = args
        q_chunk = (bi * Q_BLOCK + jnp.arange(Q_BLOCK)) // CHUNK
        mask = key_chunk[None, :] <= q_chunk[:, None]
        s = jnp.einsum('mbhqd,mbhkd->mbhqk', q_blk, kt).astype(jnp.float32)
        s = jnp.where(mask, s, -jnp.inf)
        p = jax.nn.softmax(s, axis=-1)
        a = p[0] - lam * p[1]
        return jnp.einsum('bhqk,bhkd->bhqd', a.astype(vt.dtype), vt)

    out = lax.map(block, (qb, jnp.arange(nb)))
    return out.transpose(1, 0, 3, 2, 4).reshape(B, S, H, DIFF_V_DIM)


def pool_mixer(u, w, scale):
    B, S, _ = u.shape
    ug = u.reshape(B, S, N_POOL_GROUPS, POOL_GROUP_DIM)
    ugf = ug.astype(jnp.float32)
    cs = jnp.cumsum(ugf, axis=1)
    t = jnp.arange(S)
    means = []
    for gi, win in enumerate(POOL_WINDOWS):
        c = cs[:, :, gi]
        lag = jnp.pad(c, ((0, 0), (win, 0), (0, 0)))[:, :S]
        cnt = jnp.minimum(t + 1, win).astype(jnp.float32)[None, :, None]
        means.append((c - lag) / cnt)
    d = (jnp.stack(means, axis=2) - ugf).astype(u.dtype)
    y = jnp.einsum('bsgc,gcd->bsgd', d, w)
    return y.reshape(B, S, POOL_WIDTH) * scale


def setup_inputs(seed: int = 0) -> dict:
    key = jax.random.key(seed)
    ks = jax.random.split(key, 24)
    f32 = jnp.float32

    def normal(k, shape, fan_in):
        return jax.random.normal(k, shape, f32) * (fan_in ** -0.5)

    def gain(k, shape):
        return 1.0 + 0.02 * jax.random.normal(k, shape, f32)

    return {
        "x": jax.random.normal(ks[0], (BATCH, SEQ, D_MODEL), f32),
        "ffn1_norm": gain(ks[1], (DEPTH, D_MODEL)),
        "ffn1_w_gate": normal(ks[2], (DEPTH, D_MODEL, D_FF), D_MODEL),
        "ffn1_w_up": normal(ks[3], (DEPTH, D_MODEL, D_FF), D_MODEL),
        "ffn1_w_down": normal(ks[4], (DEPTH, D_FF, D_MODEL), D_FF),
        "mix_norm": gain(ks[5], (DEPTH, D_MODEL)),
        "w_in": normal(ks[6], (DEPTH, D_MODEL, IN_WIDTH), D_MODEL),
        "lambda_q1": LAMBDA_INIT_SCALE * jax.random.normal(ks[7], (DEPTH, DIFF_HEAD_DIM), f32),
        "lambda_k1": LAMBDA_INIT_SCALE * jax.random.normal(ks[8], (DEPTH, DIFF_HEAD_DIM), f32),
        "lambda_q2": LAMBDA_INIT_SCALE * jax.random.normal(ks[9], (DEPTH, DIFF_HEAD_DIM), f32),
        "lambda_k2": LAMBDA_INIT_SCALE * jax.random.normal(ks[10], (DEPTH, DIFF_HEAD_DIM), f32),
        "subln_gain": gain(ks[11], (DEPTH, DIFF_V_DIM)),
        "pool_w": normal(ks[12], (DEPTH, N_POOL_GROUPS, POOL_GROUP_DIM, POOL_GROUP_DIM), POOL_GROUP_DIM),
        "pool_scale": 1.0 + 0.1 * jax.random.normal(ks[13], (DEPTH, POOL_WIDTH), f32),
        "w_out": normal(ks[14], (DEPTH, D_MIX, D_MODEL), D_MIX),
        "ffn2_norm": gain(ks[15], (DEPTH, D_MODEL)),
        "ffn2_w_gate": normal(ks[16], (DEPTH, D_MODEL, D_FF), D_MODEL),
        "ffn2_w_up": normal(ks[17], (DEPTH, D_MODEL, D_FF), D_MODEL),
        "ffn2_w_down": normal(ks[18], (DEPTH, D_FF, D_MODEL), D_FF),
        "final_norm": gain(ks[19], (D_MODEL,)),
    }


def reference(x, ffn1_norm, ffn1_w_gate, ffn1_w_up, ffn1_w_down, mix_norm, w_in,
              lambda_q1, lambda_k1, lambda_q2, lambda_k2, subln_gain, pool_w, pool_scale,
              w_out, ffn2_norm, ffn2_w_gate, ffn2_w_up, ffn2_w_down, final_norm):
    B, S, _ = x.shape
    cos, sin = rope_tables(S, DIFF_HEAD_DIM)
    for l in range(DEPTH):
        x = x + 0.5 * swiglu(rms_norm(x, ffn1_norm[l]), ffn1_w_gate[l], ffn1_w_up[l], ffn1_w_down[l])

        h = rms_norm(x, mix_norm[l])
        proj = h @ w_in[l]
        q = proj[..., :ATT_WIDTH].reshape(B, S, N_DIFF_HEADS, 2, DIFF_HEAD_DIM)
        k = proj[..., ATT_WIDTH:2 * ATT_WIDTH].reshape(B, S, N_DIFF_HEADS, 2, DIFF_HEAD_DIM)
        v = proj[..., 2 * ATT_WIDTH:3 * ATT_WIDTH].reshape(B, S, N_DIFF_HEADS, DIFF_V_DIM)
        u = proj[..., 3 * ATT_WIDTH:]

        q = apply_rope(q, cos, sin)
        k = apply_rope(k, cos, sin)
        lam_init = 0.8 - 0.6 * math.exp(-0.3 * l)
        lam = (jnp.exp(jnp.sum(lambda_q1[l].astype(jnp.float32) * lambda_k1[l].astype(jnp.float32)))
               - jnp.exp(jnp.sum(lambda_q2[l].astype(jnp.float32) * lambda_k2[l].astype(jnp.float32)))
               + lam_init)
        att = diff_attention(q, k, v, lam)
        att = (rms_norm(att, subln_gain[l]) * (1.0 - lam_init)).reshape(B, S, ATT_WIDTH)

        pool = pool_mixer(u, pool_w[l], pool_scale[l])

        x = x + jnp.concatenate([att.astype(x.dtype), pool.astype(x.dtype)], axis=-1) @ w_out[l]

        x = x + 0.5 * swiglu(rms_norm(x, ffn2_norm[l]), ffn2_w_gate[l], ffn2_w_up[l], ffn2_w_down[l])
    return rms_norm(x, final_norm)
```

```python
import contextlib
import numpy as np
import ml_dtypes
import concourse.bass as bass
import concourse.mybir as mybir
from concourse.bass_utils import run_bass_kernel_spmd

F32 = mybir.dt.float32
BF16 = mybir.dt.bfloat16
AF = mybir.ActivationFunctionType
ALU = mybir.AluOpType

D = 1024
DFF = 2816
NCF = 22
T = 512
NSLOT = 8
NPT = 4
EPS = 1e-6
ARENA_KIB = 31


class Item:
    __slots__ = ("eng", "fn", "kind", "deps", "marked", "count", "dsem", "dval", "waits")

    def __init__(self, eng, fn, kind):
        self.eng = eng
        self.fn = fn
        self.kind = kind
        self.deps = []
        self.marked = False
        self.count = None
        self.dsem = None
        self.dval = None
        self.waits = []


class Prog:
    ENG = ["pe", "act", "dve", "pool", "sp"]

    def __init__(self):
        self.q = {e: [] for e in self.ENG}
        self.lastw = {}
        self.readers = {}
        self.dcnt = {}

    def _deps(self, it, r, w, extra):
        deps = {}
        for k in r:
            lw = self.lastw.get(k)
            if lw is not None:
                deps[id(lw)] = lw
        for k in w:
            lw = self.lastw.get(k)
            if lw is not None:
                deps[id(lw)] = lw
            for rd in self.readers.get(k, {}).values():
                deps[id(rd)] = rd
        for x in extra:
            if x is not None:
                deps[id(x)] = x
        deps.pop(id(it), None)
        it.deps = list(deps.values())
        rk = it.dsem if it.kind == "dma" else it.eng
        for k in r:
            self.readers.setdefault(k, {})[rk] = it
        for k in w:
            self.lastw[k] = it
            self.readers[k] = {}

    def op(self, eng, fn, r=(), w=(), extra=()):
        it = Item(eng, fn, "op")
        self._deps(it, r, w, extra)
        self.q[eng].append(it)
        return it

    def dma(self, eng, sem, out, in_, r=(), w=(), extra=()):
        it = Item(eng, lambda e: e.dma_start(out=out, in_=in_), "dma")
        it.dsem = sem
        self.dcnt[sem] = self.dcnt.get(sem, 0) + 16
        it.dval = self.dcnt[sem]
        self._deps(it, r, w, extra)
        self.q[eng].append(it)
        return it

    def finalize(self):
        for e in self.ENG:
            for it in self.q[e]:
                for d in it.deps:
                    if d.kind == "op" and not (d.eng == "pe" and it.eng == "pe"):
                        d.marked = True
        for e in self.ENG:
            c = 0
            for it in self.q[e]:
                if it.kind == "op" and it.marked:
                    c += 1
                    it.count = c
        nw = 0
        for e in self.ENG:
            seen = {}
            for it in self.q[e]:
                ws = {}
                for d in it.deps:
                    if d.kind == "dma":
                        s, v = d.dsem, d.dval
                    else:
                        if d.eng == "pe" and it.eng == "pe":
                            continue
                        s, v = d.eng, d.count
                    if seen.get(s, 0) >= v:
                        continue
                    if ws.get(s, 0) < v:
                        ws[s] = v
                for s, v in ws.items():
                    seen[s] = v
                it.waits = list(ws.items())
                nw += len(it.waits)
        return nw


def build(S, n_tiles=None, dbg=False):
    NT = S // T if n_tiles is None else n_tiles
    NKT = S // 128
    nc = bass.Bass("TRN2", target_bir_lowering=False)

    def dt(name, shape, d=F32, kind="ExternalInput"):
        return nc.dram_tensor(name, shape, d, kind=kind).ap()

    x = dt("x", [S, D])
    out = dt("out", [S, D], kind="ExternalOutput")
    W = {}
    for f in (1, 2):
        W[("g", f)] = dt(f"ffn{f}_w_gate", [D, DFF])
        W[("u", f)] = dt(f"ffn{f}_w_up", [D, DFF])
        W[("d", f)] = dt(f"ffn{f}_w_down", [DFF, D])
    w_in = dt("w_in", [D, 2048])
    w_out = dt("w_out", [D, D])
    pool_w = dt("pool_w", [4, 128, 128])
    gains_d = dt("gains", [128, 24])
    gfin_d = dt("gfin", [128, D])
    lamv_d = dt("lamv", [128, 256])
    sgain_d = dt("sgain", [128, 128])
    pscale_d = dt("pscale", [128, 4])
    cosT = dt("cosT", [128, S])
    sinT = dt("sinT", [128, S])
    ident_d = dt("ident", [128, 128])
    maskm_d = dt("maskm", [128, 896], BF16)
    rcnt_d = dt("rcnt", [128, 16])
    dbg_t = {}
    if dbg:
        for nm in ("x1", "x2", "x3"):
            dbg_t[nm] = dt("dbg_" + nm, [S // T, 128, 4, D], kind="ExternalOutput")
        dbg_t["catT"] = dt("dbg_catT", [S // T, 128, 8, T], BF16, kind="ExternalOutput")
        dbg_t["KT"] = dt("dbg_KT", [128, 4, S], BF16, kind="ExternalOutput")
        dbg_t["dT"] = dt("dbg_dT", [S // T, 128, 4, T], BF16, kind="ExternalOutput")

    chunks = []
    CI = {}

    def add(key, **kw):
        CI[key] = len(chunks)
        chunks.append(kw)

    def add_ffn(f):
        for c in range(NCF):
            add(("g", f, c), kind="cols", w=W[("g", f)], c0=c * 128)
            add(("u", f, c), kind="cols", w=W[("u", f)], c0=c * 128)
        for c in range(NCF):
            add(("d", f, c), kind="rows", w=W[("d", f)], r0=c * 128)

    add_ffn(1)
    for g in range(4):
        add(("U", g), kind="cols", w=w_in, c0=1536 + g * 128)
    for v in range(4):
        add(("V", v), kind="v", v=v)
    for h in range(4):
        add(("Q", h), kind="cols", w=w_in, c0=h * 128)
        add(("QS", h), kind="cols", w=w_in, c0=h * 128, swap=True)
        add(("K", h), kind="cols", w=w_in, c0=512 + h * 128)
        add(("KS", h), kind="cols", w=w_in, c0=512 + h * 128, swap=True)
    for kc in range(8):
        add(("WO", kc), kind="rows", w=w_out, r0=kc * 128)
    add_ffn(2)
    NPER = len(chunks)
    add("PW", kind="pw")
    NCH = len(chunks)
    wbf = dt("wbf", [NCH, 128, 1024], BF16, kind="Internal")

    P = Prog()
    es = contextlib.ExitStack()
    with es:
        def sb(name, shape, d):
            return es.enter_context(nc.sbuf_tensor(name, shape, d))

        KT = sb("KT", [128, 4, S], BF16)
        VC = sb("VC", [128, NKT, 4, 130], BF16)
        xres = sb("xres", [128, 4, D], F32)
        xnT = sb("xnT", [128, 8, T], BF16)
        ring = sb("ring", [128, NSLOT, 1024], BF16)
        arena = sb("arena", [128, ARENA_KIB * 256], F32)
        rstd_b = sb("rstd_b", [128, T], F32)
        ident = sb("ident_s", [128, 128], F32)
        identb = sb("identb", [128, 128], BF16)
        zerosb = sb("zerosb", [128, 128], BF16)
        ones32 = sb("ones32", [128, 128], F32)
        maskm = sb("maskm_s", [128, 896], BF16)
        sgain08 = sb("sgain08", [128, 128], F32)
        pwb = sb("pwb", [128, 512], BF16)
        gains = sb("gains_s", [128, 24], F32)
        pscale = sb("pscale_s", [128, 4], F32)
        rcnt = sb("rcnt_s", [128, 16], F32)
        halo = sb("halo", [128, 4, 16], F32)
        small = sb("small", [128, 64], F32)
        junkS = sb("junkS", [128, 128], BF16)
        tmpf = sb("tmpf", [128, 16], F32)
        psb = [es.enter_context(nc.psum_tensor(f"ps{b}", [128, 512], F32)) for b in range(8)]

        ssq = small[:, 0:4]
        mse = small[:, 4:8]
        rstd = small[:, 8:12]
        neghalf = small[:, 12:20]
        rr = small[:, 20:28]
        rn = small[:, 28:32]
        ssa = small[:, 32:36]
        msa = small[:, 36:40]
        rsa = small[:, 40:44]
        lsc = small[:, 44:52]
        neglam = small[:, 49:50]

        def av(off, nbytes, d):
            a = arena[:, off // 4:(off + nbytes) // 4]
            return a if d == F32 else a.bitcast(d)

        def blk(off, nbytes):
            return [("A", i) for i in range(off // 1024, (off + nbytes + 1023) // 1024)]

        KB = 1024
        gT = av(0, 22 * KB, BF16).rearrange("p (c n) -> p c n", n=T)
        sg = [av(22 * KB + i * 2 * KB, 2 * KB, F32) for i in range(2)]
        stage = av(0, 16 * KB, F32).rearrange("p (s d) -> p s d", d=D)
        R0 = av(27 * KB, 2 * KB, F32)
        R1 = av(29 * KB, 2 * KB, F32)
        gfinb = av(27 * KB, 4 * KB, F32)
        kR0, kR1 = blk(27 * KB, 2 * KB), blk(29 * KB, 2 * KB)
        QT = av(0, 8 * KB, BF16).rearrange("p (h m n) -> p h m n", h=4, m=2)
        poolT = av(8 * KB, 4 * KB, BF16).rearrange("p (g n) -> p g n", n=T)
        dT = av(12 * KB, 4 * KB, BF16).rearrange("p (g n) -> p g n", n=T)
        PT = av(16 * KB, NPT * KB, BF16).rearrange("p (b n) -> p b n", n=T)
        Dm = av(16 * KB, 2 * KB, F32).rearrange("p (s n) -> p s n", n=128)
        UB = 2112
        uoff = [20 * KB, 20 * KB + UB, 20 * KB + 2 * UB]
        ubuf = [av(o, UB, F32) for o in uoff]
        kub = [blk(o, UB) for o in uoff]
        aB = R0.rearrange("p (q n) -> p q n", n=128)
        aA = R1[:, 0:256].rearrange("p (q n) -> p q n", n=128)
        an = R1[:, 256:512].bitcast(BF16).rearrange("p (q n) -> p q n", n=128)
        st32 = [av(b * 4 * KB, 4 * KB, F32) for b in range(4)]
        st16 = [av(16 * KB + b * 2 * KB, 2 * KB, BF16) for b in range(4)]
        junk_sb = av(18 * KB, 2 * KB, BF16)
        kjunk = blk(18 * KB, 2 * KB)
        ps3b = psb[3][:].bitcast(BF16)

        def kps(b):
            return ("ps", b)

        setup_items = []
        for (dst, src, key) in [
            (ident[:], ident_d, "ident"), (maskm[:], maskm_d, "maskm"), (gains[:], gains_d, "gains"),
            (pscale[:], pscale_d, "pscale"), (rcnt[:], rcnt_d, "rcnt"),
            (sgain08[:], sgain_d, "sgain08"), (R0[:, 0:256], lamv_d, "lamv"),
        ]:
            setup_items.append(P.dma("sp", "setup", out=dst, in_=src, w=[key] if key != "lamv" else kR0))
        fence = P.op("sp", lambda e: e.nop(), extra=setup_items)
        for key in ["ident", "maskm", "gains", "pscale", "rcnt", "sgain08"] + kR0:
            P.lastw[key] = setup_items[-1]
        P.op("dve", lambda e: e.memset(zerosb[:], 0.0), w=["zerosb"])
        P.op("dve", lambda e: e.memset(ones32[:], 1.0), w=["ones32"])
        P.op("dve", lambda e: e.memset(neghalf, -0.5), w=["neghalf"])
        P.op("dve", lambda e: e.memset(halo[:], 0.0), w=["halo"])
        P.op("pool", lambda e: e.memset(QT[:, :, :, :], 0.0), w=blk(0, 8 * KB))
        P.op("pool", lambda e: e.memset(VC[:, :, :, 128:129], 1.0), w=["VCinit"])
        P.op("pool", lambda e: e.memset(VC[:, :, :, 129:130], 0.0), w=["VCinit"])
        P.op("dve", lambda e: e.tensor_copy(out=identb[:], in_=ident[:]), r=["ident"], w=["identb"])
        P.op("dve", lambda e: e.tensor_scalar(out=sgain08[:], in0=sgain08[:], scalar1=0.8, scalar2=None,
                                              op0=ALU.mult), r=["sgain08"], w=["sgain08"])
        lam_in = R0[:, 0:256]
        lam_t = R0[:, 256:384]
        P.op("dve", lambda e: e.tensor_tensor(out=lam_t[:, 0:64], in0=lam_in[:, 0:64], in1=lam_in[:, 64:128],
                                              op=ALU.mult), r=kR0, w=kR0)
        P.op("dve", lambda e: e.tensor_tensor(out=lam_t[:, 64:128], in0=lam_in[:, 128:192], in1=lam_in[:, 192:256],
                                              op=ALU.mult), r=kR0, w=kR0)
        P.op("act", lambda e: e.activation(out=junkS[:, 0:64], in_=lam_t[:, 0:64], func=AF.Copy,
                                           accum_out=lsc[:, 0:1]), r=kR0, w=["junkS", "l1"])
        P.op("act", lambda e: e.activation(out=junkS[:, 64:128], in_=lam_t[:, 64:128], func=AF.Copy,
                                           accum_out=lsc[:, 1:2]), r=kR0, w=["junkS", "l2"])
        P.op("act", lambda e: e.activation(out=lsc[:, 2:4], in_=lsc[:, 0:2], func=AF.Exp), r=["l1", "l2"], w=["e12"])
        P.op("dve", lambda e: e.tensor_tensor(out=lsc[:, 4:5], in0=lsc[:, 3:4], in1=lsc[:, 2:3], op=ALU.subtract),
             r=["e12"], w=["lamd"])
        P.op("dve", lambda e: e.tensor_scalar(out=neglam, in0=lsc[:, 4:5], scalar1=-0.2, scalar2=None, op0=ALU.add),
             r=["lamd"], w=["neglam"])

        NB32, NB16 = 4, 6
        if S >= 8192:
            def ktreg(h, ti, ntl, d):
                a_ = KT[:, h, ti * 512:(ti + ntl) * 512]
                return (a_ if d == BF16 else a_.bitcast(d)), [("KT", h, tt) for tt in range(ti, ti + ntl)]
            st32r = [ktreg(0, 1, 4, F32), ktreg(0, 5, 4, F32), ktreg(1, 1, 4, F32), ktreg(1, 5, 4, F32)]
            st16r = [ktreg(2, 1, 2, BF16), ktreg(2, 3, 2, BF16), ktreg(2, 5, 2, BF16), ktreg(2, 7, 2, BF16),
                     ktreg(3, 1, 2, BF16), ktreg(3, 3, 2, BF16)]
        else:
            stg32 = sb("stg32", [128, NB32, 1024], F32)
            stg16 = sb("stg16", [128, NB16, 1024], BF16)
            st32r = [(stg32[:, b, :], [("stg32", b)]) for b in range(NB32)]
            st16r = [(stg16[:, b, :], [("stg16", b)]) for b in range(NB16)]

        def convert_chunk(ci, b32, dst16, kdst16):
            ch = chunks[ci]
            s32, k32 = st32r[b32]
            kind = ch["kind"]
            ncols = 1024
            if kind == "cols":
                src = ch["w"].rearrange("(kc p) n -> p kc n", p=128)[:, :, ch["c0"]:ch["c0"] + 128]
                dst = s32.rearrange("p (kc n) -> p kc n", n=128)
            elif kind == "rows":
                src = ch["w"][ch["r0"]:ch["r0"] + 128, :]
                dst = s32
            elif kind == "v":
                v = ch["v"]
                src = w_in.rearrange("(kc p) n -> p kc n", p=128)[:, 2 * v:2 * v + 2, 1024:1536]
                dst = s32.rearrange("p (kc n) -> p kc n", n=512)
            else:
                src = pool_w.rearrange("g p d -> p g d")
                dst = s32[:, 0:512].rearrange("p (g d) -> p g d", d=128)
                ncols = 512
            P.dma("sp", f"pl{b32}", out=dst, in_=src, w=k32)
            if ch.get("swap"):
                i3 = s32.rearrange("p (a d) -> p a d", d=64)
                o3 = dst16.rearrange("p (a d) -> p a d", d=64)
                P.op("pool", lambda e: e.tensor_copy(out=o3[:, :, 0:32], in_=i3[:, :, 32:64]), r=k32, w=kdst16)
                P.op("pool", lambda e: e.tensor_copy(out=o3[:, :, 32:64], in_=i3[:, :, 0:32]), r=k32, w=kdst16)
            else:
                P.op("pool", lambda e: e.tensor_copy(out=dst16[:, 0:ncols], in_=s32[:, 0:ncols]), r=k32, w=kdst16)

        conv_n = [0]
        convert_chunk(CI["PW"], 0, pwb[:], ["pwb"])
        conv_n[0] += 1

        seq = [ci for _ in range(NT) for ci in range(NPER)]
        st = dict(issued=0, consumed=0, nxt=0)
        chunk_ap = {}

        def prefetch():
            while st["issued"] < len(seq):
                n = st["issued"]
                depth = NB16 if n < NPER else NSLOT
                if n >= st["consumed"] + depth:
                    break
                ci = seq[n]
                if n < NPER:
                    ap16, k16 = st16r[n % NB16]
                    convert_chunk(ci, conv_n[0] % NB32, ap16, k16)
                    conv_n[0] += 1
                    P.dma("sp", f"pst{n % NB16}", out=wbf[ci, :, :], in_=ap16, r=k16, w=[("wbf", ci)])
                    chunk_ap[n] = (ap16, k16)
                else:
                    slot = n % NSLOT
                    P.dma("sp", f"slot{slot}", out=ring[:, slot, :], in_=wbf[ci, :, :],
                          r=[("wbf", ci)], w=[("slot", slot)])
                    chunk_ap[n] = (ring[:, slot, :], [("slot", slot)])
                st["issued"] += 1

        def need(key):
            n = st["nxt"]
            assert seq[n] == CI[key], (key, n, seq[n], CI[key])
            st["nxt"] += 1
            if n >= st["issued"]:
                prefetch()
            assert n < st["issued"]
            return chunk_ap.pop(n)

        def done(k=1):
            st["consumed"] += k
            prefetch()

        def mm(o, lhsT, rhs, start, stop, r, w, **kw):
            return P.op("pe", lambda e: e.matmul(o, lhsT, rhs, start=start, stop=stop, **kw), r=r, w=w)

        def rms_stats(n_out=4):
            for s in range(4):
                P.op("act", lambda e, s=s: e.activation(out=junk_sb, in_=xres[:, s, :], func=AF.Square,
                                                        accum_out=ssq[:, s:s + 1]),
                     r=[("xres", s)], w=kjunk + [("ssq", s)])
            P.op("dve", lambda e: e.tensor_scalar(out=mse, in0=ssq, scalar1=1.0 / D, scalar2=EPS,
                                                  op0=ALU.mult, op1=ALU.add),
                 r=[("ssq", s) for s in range(4)], w=["mse"])
            P.op("pool", lambda e: e.tensor_tensor(out=rstd, in0=mse, in1=neghalf[:, 0:4], op=ALU.pow),
                 r=["mse", "neghalf"], w=["rstd"])

        def norm_to_xnT(gi):
            rms_stats()
            kDm = blk(16 * KB, 2 * KB)

            def transposes(kc):
                bank = 4 + kc % 2
                for s in range(4):
                    P.op("pe", lambda e, s=s, kc=kc, bank=bank: e.transpose(
                        out=psb[bank][:, s * 128:(s + 1) * 128], in_=xres[:, s, kc * 128:(kc + 1) * 128],
                        identity=ident[:]), r=[("xres", s), "ident"], w=[kps(bank)])

            def evac(kc):
                bank = 4 + kc % 2
                P.op("dve", lambda e, kc=kc, bank=bank: e.scalar_tensor_tensor(
                    out=xnT[:, kc, :], in0=psb[bank][:], scalar=gains[:, gi * 8 + kc:gi * 8 + kc + 1],
                    in1=rstd_b[:], op0=ALU.mult, op1=ALU.mult),
                     r=[kps(bank), "rstd_b", "gains"], w=[("xnT", kc)])

            transposes(0)
            transposes(1)
            for s in range(4):
                P.op("dve", lambda e, s=s: e.tensor_scalar(out=Dm[:, s, :], in0=ident[:], scalar1=rstd[:, s:s + 1],
                                                           scalar2=None, op0=ALU.mult),
                     r=["ident", "rstd"], w=kDm)
            mm(psb[6][:], ones32[:], Dm.rearrange("p s n -> p (s n)"), True, True, r=kDm + ["ones32"], w=[kps(6)])
            P.op("act", lambda e: e.activation(out=rstd_b[:], in_=psb[6][:], func=AF.Copy), r=[kps(6)], w=["rstd_b"])
            for kc in range(8):
                evac(kc)
                if kc + 2 < 8:
                    transposes(kc + 2)

        def ffn(f, gi):
            norm_to_xnT(gi)
            for c in range(NCF):
                par = c % 2
                for (nm, bank) in (("g", par), ("u", 2 + par)):
                    wap, wk = need((nm, f, c))
                    for kc in range(8):
                        mm(psb[bank][:], wap[:, kc * 128:(kc + 1) * 128], xnT[:, kc, :], kc == 0, kc == 7,
                           r=wk + [("xnT", kc)], w=[kps(bank)])
                    done()
                ksg = blk(22 * KB + par * 2 * KB, 2 * KB)
                P.op("act", lambda e, par=par: e.activation(out=sg[par], in_=psb[par][:], func=AF.Silu),
                     r=[kps(par)], w=ksg)
                P.op("dve", lambda e, par=par, c=c: e.tensor_tensor(out=gT[:, c, :], in0=psb[2 + par][:], in1=sg[par],
                                                                    op=ALU.mult),
                     r=[kps(2 + par)] + ksg, w=[("A", c)])
            for c in range(NCF):
                wap, wk = need(("d", f, c))
                for s in range(4):
                    for hf in range(2):
                        b = s * 2 + hf
                        mm(psb[b][:], gT[:, c, s * 128:(s + 1) * 128], wap[:, hf * 512:(hf + 1) * 512],
                           c == 0, c == NCF - 1, r=wk + [("A", c)], w=[kps(b)])
                done()
            for s in range(4):
                for hf in range(2):
                    b = s * 2 + hf
                    xs = xres[:, s, hf * 512:(hf + 1) * 512]
                    P.op("dve", lambda e, b=b, xs=xs: e.scalar_tensor_tensor(
                        out=xs, in0=psb[b][:], scalar=0.5, in1=xs, op0=ALU.mult, op1=ALU.add),
                         r=[kps(b), ("xres", s)], w=[("xres", s)])

        cnt = [0]

        def attention_tile(t):
            nk = 4 * (t + 1)
            per_head = 2 * nk
            units = [(h, j, m) for h in range(4) for j in range(nk) for m in range(2)]
            n = len(units)
            LA = 2
            info = {}

            def kQ(h, m):
                return ("A", 2 * h + m)

            def qk(u):
                h, j, m = units[u]
                diag = j >= 4 * t
                jj = j - 4 * t
                sbk = 4 + cnt[0] % 3
                pb = cnt[0] % NPT
                cnt[0] += 1
                info[u] = (sbk, pb)
                mm(psb[sbk][:], KT[:, h, j * 128:(j + 1) * 128], QT[:, h, m, :], True, not diag,
                   r=[("KT", h, j // 4), kQ(h, m)], w=[kps(sbk)])
                if diag:
                    mm(psb[sbk][:], identb[:], maskm[:, 384 - 128 * jj:896 - 128 * jj], False, True,
                       r=["identb", "maskm"], w=[kps(sbk)])

            def ex(u):
                sbk, pb = info[u]
                P.op("act", lambda e, sbk=sbk, pb=pb: e.activation(out=PT[:, pb, :], in_=psb[sbk][:],
                                                                   func=AF.Exp, scale=0.125),
                     r=[kps(sbk)], w=[("A", 16 + pb)])

            def av(u):
                h, j, m = units[u]
                sbk, pb = info[u]
                diag = j >= 4 * t
                jj = j - 4 * t
                for qs in range(jj if diag else 0, 4):
                    a = qs * 2 + m
                    bank, off = a // 3, (a % 3) * 130
                    last = (j == 4 * t + qs)
                    mm(psb[bank][:, off:off + 130], PT[:, pb, qs * 128:(qs + 1) * 128], VC[:, j, h, :],
                       False, last, r=[("A", 16 + pb), ("VC", j), "VCinit"], w=[kps(bank)], skip_group_check=True)

            def zero_O():
                for b in range(3):
                    mm(psb[b][:], zerosb[:], maskm[:, 0:512], True, False, r=["zerosb", "maskm"], w=[kps(b)],
                       skip_group_check=True)

            def epiA(h):
                for qs in range(4):
                    a1, a2 = qs * 2, qs * 2 + 1
                    b1, o1 = a1 // 3, (a1 % 3) * 130
                    b2, o2 = a2 // 3, (a2 % 3) * 130
                    P.op("dve", lambda e, qs=qs, b1=b1, o1=o1: e.reciprocal(out=rr[:, 2 * qs:2 * qs + 1],
                                                                            in_=psb[b1][:, o1 + 128:o1 + 129]),
                         r=[kps(b1)], w=[("rr", 2 * qs)])
                    P.op("dve", lambda e, qs=qs, b2=b2, o2=o2: e.reciprocal(out=rr[:, 2 * qs + 1:2 * qs + 2],
                                                                            in_=psb[b2][:, o2 + 128:o2 + 129]),
                         r=[kps(b2)], w=[("rr", 2 * qs + 1)])
                    P.op("dve", lambda e, qs=qs: e.tensor_scalar(out=rn[:, qs:qs + 1],
                                                                 in0=rr[:, 2 * qs + 1:2 * qs + 2],
                                                                 scalar1=neglam, scalar2=None, op0=ALU.mult),
                         r=[("rr", 2 * qs + 1), "neglam"], w=[("rn", qs)])
                    P.op("act", lambda e, qs=qs, b1=b1, o1=o1: e.activation(out=aA[:, qs % 2, :],
                                                                            in_=psb[b1][:, o1:o1 + 128],
                                                                            func=AF.Copy,
                                                                            scale=rr[:, 2 * qs:2 * qs + 1]),
                         r=[kps(b1), ("rr", 2 * qs)], w=[("aA", qs % 2)] + (kR1 if (h == 0 and qs == 0) else []))
                    P.op("dve", lambda e, qs=qs, b2=b2, o2=o2: e.scalar_tensor_tensor(
                        out=aB[:, qs, :], in0=psb[b2][:, o2:o2 + 128], scalar=rn[:, qs:qs + 1],
                        in1=aA[:, qs % 2, :], op0=ALU.mult, op1=ALU.add),
                         r=[kps(b2), ("rn", qs), ("aA", qs % 2)], w=[("aB", qs)] + (kR0 if (h == 0 and qs == 0) else []))
                    P.op("act", lambda e, qs=qs: e.activation(out=junkS[:], in_=aB[:, qs, :], func=AF.Square,
                                                              accum_out=ssa[:, qs:qs + 1]),
                         r=[("aB", qs)], w=["junkS", ("ssa", qs)])
                P.op("dve", lambda e: e.tensor_scalar(out=msa, in0=ssa, scalar1=1.0 / 128, scalar2=EPS,
                                                      op0=ALU.mult, op1=ALU.add),
                     r=[("ssa", q) for q in range(4)], w=["msa"])
                P.op("pool", lambda e: e.tensor_tensor(out=rsa, in0=msa, in1=neghalf[:, 0:4], op=ALU.pow),
                     r=["msa", "neghalf"], w=["rsa"])
                for qs in range(4):
                    P.op("dve", lambda e, qs=qs: e.scalar_tensor_tensor(
                        out=an[:, qs, :], in0=aB[:, qs, :], scalar=rsa[:, qs:qs + 1], in1=sgain08[:],
                        op0=ALU.mult, op1=ALU.mult),
                         r=[("aB", qs), "rsa", "sgain08"], w=[("an", qs)])

            def epiB(h):
                for qs in range(4):
                    P.op("pe", lambda e, qs=qs: e.transpose(out=ps3b[:, qs * 128:(qs + 1) * 128], in_=an[:, qs, :],
                                                            identity=identb[:]),
                         r=[("an", qs), "identb"], w=[kps(3)])
                P.op("act", lambda e: e.activation(out=QT[:, h, 0, :], in_=ps3b[:, 0:512], func=AF.Copy),
                     r=[kps(3)], w=[kQ(h, 0)])

            pending = [None]
            for u in range(min(LA, n)):
                qk(u)
            for u in range(n):
                h, j, m = units[u]
                pos = u % per_head
                if pos == 0:
                    zero_O()
                if u + LA < n:
                    qk(u + LA)
                ex(u)
                av(u)
                if pending[0] is not None and pos == min(7, per_head - 1):
                    epiB(pending[0])
                    pending[0] = None
                if pos == per_head - 1:
                    epiA(h)
                    pending[0] = h
            if pending[0] is not None:
                epiB(pending[0])

        def mixer(t):
            norm_to_xnT(1)
            cosb, sinb = R0, R1
            P.dma("sp", "cos", out=cosb, in_=cosT[:, t * T:(t + 1) * T], w=kR0)
            P.dma("sp", "sin", out=sinb, in_=sinT[:, t * T:(t + 1) * T], w=kR1)
            P.op("pool", lambda e: e.memset(QT[64:128, :, 0, :], 0.0), w=[("A", 2 * h) for h in range(4)])
            P.op("pool", lambda e: e.memset(QT[0:64, :, 1, :], 0.0), w=[("A", 2 * h + 1) for h in range(4)])
            uA, uB, uC = ubuf
            for g in range(4):
                wap, wk = need(("U", g))
                for kc in range(8):
                    mm(psb[4 + g][:], wap[:, kc * 128:(kc + 1) * 128], xnT[:, kc, :], kc == 0, kc == 7,
                       r=wk + [("xnT", kc)], w=[kps(4 + g)])
                done()
                P.op("act", lambda e, g=g: e.activation(out=uA[:, 16:528], in_=psb[4 + g][:], func=AF.Copy),
                     r=[kps(4 + g)], w=kub[0])
                P.op("pool", lambda e, g=g: e.tensor_copy(out=uA[:, 0:16], in_=halo[:, g, :]),
                     r=[("halo", g), "halo"], w=kub[0])
                P.op("pool", lambda e, g=g: e.tensor_copy(out=halo[:, g, :], in_=uA[:, 512:528]),
                     r=kub[0], w=[("halo", g)])
                cur, kcur = uA, kub[0]
                for lvl in range(g + 1):
                    sh = 1 << lvl
                    lo = 2 * sh - 1
                    dst, kdst = (uB, kub[1]) if lvl % 2 == 0 else (uC, kub[2])
                    P.op("pool", lambda e, cur=cur, dst=dst, lo=lo, sh=sh: e.tensor_tensor(
                        out=dst[:, lo:528], in0=cur[:, lo:528], in1=cur[:, lo - sh:528 - sh], op=ALU.add),
                         r=kcur, w=kdst)
                    cur, kcur = dst, kdst
                win = 2 << g
                P.op("dve", lambda e, g=g, cur=cur, win=win: e.scalar_tensor_tensor(
                    out=dT[:, g, :], in0=cur[:, 16:528], scalar=1.0 / win, in1=uA[:, 16:528],
                    op0=ALU.mult, op1=ALU.subtract), r=kcur + kub[0], w=[("A", 12 + g)])
                if t == 0:
                    n = win - 1
                    P.op("dve", lambda e, cur=cur, n=n: e.tensor_tensor(out=tmpf[:, 0:n], in0=cur[:, 16:16 + n],
                                                                        in1=rcnt[:, 0:n], op=ALU.mult),
                         r=kcur + ["rcnt"], w=["tmpf"])
                    P.op("dve", lambda e, g=g, n=n: e.tensor_tensor(out=dT[:, g, 0:n], in0=tmpf[:, 0:n],
                                                                    in1=uA[:, 16:16 + n], op=ALU.subtract),
                         r=["tmpf"] + kub[0], w=[("A", 12 + g)])
            vsl = [need(("V", v)) for v in range(4)]
            for v in range(4):
                for kk in range(2):
                    kc = 2 * v + kk
                    for s in range(4):
                        mm(psb[s][:], xnT[:, kc, s * 128:(s + 1) * 128], vsl[v][0][:, kk * 512:(kk + 1) * 512],
                           kc == 0, kc == 7, r=vsl[v][1] + [("xnT", kc)], w=[kps(s)])
                done()
            for s in range(4):
                P.op("act", lambda e, s=s: e.activation(out=VC[:, 4 * t + s, :, 0:128],
                                                        in_=psb[s][:].rearrange("p (h d) -> p h d", h=4),
                                                        func=AF.Copy),
                     r=[kps(s)], w=[("VC", 4 * t + s)])
            for h in range(4):
                for (nm, bank) in (("Q", 0), ("QS", 1), ("K", 2), ("KS", 3)):
                    wap, wk = need((nm, h))
                    for kc in range(8):
                        mm(psb[bank][:], wap[:, kc * 128:(kc + 1) * 128], xnT[:, kc, :], kc == 0, kc == 7,
                           r=wk + [("xnT", kc)], w=[kps(bank)])
                    done()
                for (b0, isq) in ((0, True), (2, False)):
                    P.op("dve", lambda e, b0=b0: e.tensor_tensor(out=psb[b0][:], in0=psb[b0][:], in1=cosb,
                                                                 op=ALU.mult), r=[kps(b0)] + kR0, w=[kps(b0)])
                    P.op("dve", lambda e, b0=b0: e.tensor_tensor(out=rstd_b[:], in0=psb[b0 + 1][:], in1=sinb,
                                                                 op=ALU.mult), r=[kps(b0 + 1)] + kR1, w=["rstd_b"])
                    if isq:
                        P.op("dve", lambda e, h=h: e.tensor_tensor(out=QT[0:64, h, 0, :], in0=psb[0][0:64, :],
                                                                   in1=rstd_b[0:64, :], op=ALU.add),
                             r=[kps(0), "rstd_b"], w=[("A", 2 * h)])
                        P.op("dve", lambda e, h=h: e.tensor_tensor(out=QT[64:128, h, 1, :], in0=psb[0][64:128, :],
                                                                   in1=rstd_b[64:128, :], op=ALU.add),
                             r=[kps(0), "rstd_b"], w=[("A", 2 * h + 1)])
                    else:
                        P.op("dve", lambda e, h=h: e.tensor_tensor(out=KT[:, h, t * T:(t + 1) * T], in0=psb[2][:],
                                                                   in1=rstd_b[:], op=ALU.add),
                             r=[kps(2), "rstd_b"], w=[("KT", h, t)])
            attention_tile(t)
            for g in range(4):
                bank = 4 + g % 2
                mm(psb[bank][:], pwb[:, g * 128:(g + 1) * 128], dT[:, g, :], True, True,
                   r=["pwb", ("A", 12 + g)], w=[kps(bank)])
                P.op("act", lambda e, g=g, bank=bank: e.activation(out=poolT[:, g, :], in_=psb[bank][:], func=AF.Copy,
                                                                   scale=pscale[:, g:g + 1]),
                     r=[kps(bank), "pscale"], w=[("A", 8 + g)])
            if dbg:
                P.dma("sp", "dbg", out=dbg_t["catT"][t][:, 0:4, :], in_=QT[:, :, 0, :], r=[("A", i) for i in range(8)])
                P.dma("sp", "dbg", out=dbg_t["catT"][t][:, 4:8, :], in_=poolT[:, :, :], r=[("A", 8 + i) for i in range(4)])
                P.dma("sp", "dbg", out=dbg_t["dT"][t], in_=dT[:, :, :], r=[("A", 12 + i) for i in range(4)])
            for kc in range(8):
                wap, wk = need(("WO", kc))
                lhs = QT[:, kc, 0, :] if kc < 4 else poolT[:, kc - 4, :]
                kl = ("A", 2 * kc) if kc < 4 else ("A", 8 + kc - 4)
                for s in range(4):
                    for hf in range(2):
                        b = s * 2 + hf
                        mm(psb[b][:], lhs[:, s * 128:(s + 1) * 128], wap[:, hf * 512:(hf + 1) * 512],
                           kc == 0, kc == 7, r=wk + [kl], w=[kps(b)])
                done()
            for s in range(4):
                for hf in range(2):
                    b = s * 2 + hf
                    xs = xres[:, s, hf * 512:(hf + 1) * 512]
                    P.op("dve", lambda e, b=b, xs=xs: e.tensor_tensor(out=xs, in0=psb[b][:], in1=xs, op=ALU.add),
                         r=[kps(b), ("xres", s)], w=[("xres", s)])

        last_store = [None]

        def final(t):
            P.dma("sp", "gfin", out=gfinb, in_=gfin_d,
                  w=kR0 + kR1 + [("aA", 0), ("aA", 1)] + [("aB", q) for q in range(4)] + [("an", q) for q in range(4)])
            rms_stats()
            for s in range(4):
                P.op("dve", lambda e, s=s: e.scalar_tensor_tensor(
                    out=stage[:, s, :], in0=xres[:, s, :], scalar=rstd[:, s:s + 1], in1=gfinb,
                    op0=ALU.mult, op1=ALU.mult),
                     r=[("xres", s), "rstd"] + kR0 + kR1, w=blk(s * 4 * KB, 4 * KB))
            last_store[0] = P.dma("act", "ost",
                                  out=out[t * T:(t + 1) * T, :].rearrange("(s p) d -> p s d", p=128),
                                  in_=stage, r=blk(0, 16 * KB))

        for t in range(NT):
            for s in range(4):
                P.dma("sp", f"x{s}", out=xres[:, s, :], in_=x[t * T + s * 128:t * T + (s + 1) * 128, :],
                      w=[("xres", s)])
            ffn(1, 0)
            if dbg:
                P.dma("sp", "dbg", out=dbg_t["x1"][t], in_=xres[:, :, :], r=[("xres", s) for s in range(4)])
            mixer(t)
            if dbg:
                P.dma("sp", "dbg", out=dbg_t["x2"][t], in_=xres[:, :, :], r=[("xres", s) for s in range(4)])
            ffn(2, 2)
            if dbg:
                P.dma("sp", "dbg", out=dbg_t["x3"][t], in_=xres[:, :, :], r=[("xres", s) for s in range(4)])
            final(t)
        if dbg:
            last_store[0] = P.dma("sp", "dbg", out=dbg_t["KT"], in_=KT[:, :, :], r=[("KT", h, t) for h in range(4) for t in range(NT)])
        P.op("sp", lambda e: e.nop(), extra=[last_store[0]])
        P.op("act", lambda e: e.nop(), extra=[last_store[0]])

        nwaits = P.finalize()
        sem_names = set(P.ENG) | set(P.dcnt.keys())
        sems = {n: es.enter_context(nc.semaphore(n)) for n in sorted(sem_names)}

        def replay(name):
            def run(e):
                for it in P.q[name]:
                    for (sn, v) in it.waits:
                        e.wait_ge(sems[sn], v)
                    ins = it.fn(e)
                    if it.kind == "dma":
                        ins.then_inc(sems[it.dsem], 16)
                    elif it.marked:
                        ins.then_inc(sems[it.eng], 1)
            return run

        with nc.Block() as block:
            block.tensor(replay("pe"))
            block.scalar(replay("act"))
            block.vector(replay("dve"))
            block.gpsimd(replay("pool"))
            block.sync(replay("sp"))
    nc._stats = {e: len(P.q[e]) for e in P.ENG}
    nc._stats["waits"] = nwaits
    return nc


def host_consts(S):
    inv_freq = (1.0 / (np.float32(10000.0) ** (np.arange(0, 64, 2, dtype=np.float32) / np.float32(64)))).astype(np.float32)
    pos = np.arange(S, dtype=np.float32)
    ang = (pos[:, None] * inv_freq[None, :]).astype(np.float32)
    c = np.cos(ang).astype(np.float32).T
    s = np.sin(ang).astype(np.float32).T
    cos64 = np.concatenate([c, c], axis=0)
    sin64 = np.concatenate([-s, s], axis=0)
    cosT = np.ascontiguousarray(np.concatenate([cos64, cos64], axis=0))
    sinT = np.ascontiguousarray(np.concatenate([sin64, sin64], axis=0))
    p = np.arange(128)[:, None]
    cc = np.arange(896)[None, :]
    maskm = np.where(cc >= 384 + 64 * (p >= 64), 0.0, -30000.0).astype(ml_dtypes.bfloat16)
    rcnt = np.broadcast_to((1.0 / np.arange(1, 17, dtype=np.float32))[None, :], (128, 16)).copy()
    return dict(cosT=cosT, sinT=sinT, ident=np.eye(128, dtype=np.float32), maskm=maskm, rcnt=rcnt)


def host_params(inp):
    f32 = lambda a: np.ascontiguousarray(np.asarray(a, dtype=np.float32))
    g3 = [f32(inp[k])[0].reshape(8, 128).T for k in ("ffn1_norm", "mix_norm", "ffn2_norm")]
    m = dict(
        gains=np.ascontiguousarray(np.concatenate(g3, axis=1)),
        gfin=np.ascontiguousarray(np.broadcast_to(f32(inp["final_norm"])[None, :], (128, D))),
        lamv=np.ascontiguousarray(np.broadcast_to(np.concatenate(
            [f32(inp[k])[0] for k in ("lambda_q1", "lambda_k1", "lambda_q2", "lambda_k2")])[None, :], (128, 256))),
        sgain=np.ascontiguousarray(np.broadcast_to(f32(inp["subln_gain"])[0][None, :], (128, 128))),
        pscale=np.ascontiguousarray(f32(inp["pool_scale"])[0].reshape(4, 128).T),
        w_in=f32(inp["w_in"])[0], w_out=f32(inp["w_out"])[0], pool_w=f32(inp["pool_w"])[0],
    )
    for f in (1, 2):
        for nm in ("gate", "up", "down"):
            m[f"ffn{f}_w_{nm}"] = f32(inp[f"ffn{f}_w_{nm}"])[0]
    return m


def kernel(**inputs):
    x = np.asarray(inputs["x"], dtype=np.float32)
    B, S, _ = x.shape
    nc = build(S)
    shared = host_params(inputs)
    shared.update(host_consts(S))
    in_maps = []
    for b in range(B):
        m = dict(shared)
        m["x"] = np.ascontiguousarray(x[b])
        in_maps.append(m)
    res = run_bass_kernel_spmd(nc, in_maps, core_ids=list(range(B)))
    return np.stack([np.asarray(r["out"], dtype=np.float32) for r in res.results], axis=0)
```

```python
import contextlib
import numpy as np
import ml_dtypes
import concourse.bass as bass
import concourse.mybir as mybir
from concourse.bass_utils import run_bass_kernel_spmd

F32 = mybir.dt.float32
BF16 = mybir.dt.bfloat16
AF = mybir.ActivationFunctionType
ALU = mybir.AluOpType

D = 1024
DFF = 2816
NCF = 22
T = 512
NSLOT = 8
NPT = 4
EPS = 1e-6
ARENA_KIB = 31


class Item:
    __slots__ = ("eng", "fn", "kind", "deps", "marked", "count", "dsem", "dval", "waits")

    def __init__(self, eng, fn, kind):
        self.eng = eng
        self.fn = fn
        self.kind = kind
        self.deps = []
        self.marked = False
        self.count = None
        self.dsem = None
        self.dval = None
        self.waits = []


class Prog:
    ENG = ["pe", "act", "dve", "pool", "sp"]

    def __init__(self):
        self.q = {e: [] for e in self.ENG}
        self.lastw = {}
        self.readers = {}
        self.dcnt = {}

    def _deps(self, it, r, w, extra):
        deps = {}
        for k in r:
            lw = self.lastw.get(k)
            if lw is not None:
                deps[id(lw)] = lw
        for k in w:
            lw = self.lastw.get(k)
            if lw is not None:
                deps[id(lw)] = lw
            for rd in self.readers.get(k, {}).values():
                deps[id(rd)] = rd
        for x in extra:
            if x is not None:
                deps[id(x)] = x
        deps.pop(id(it), None)
        it.deps = list(deps.values())
        rk = it.dsem if it.kind == "dma" else it.eng
        for k in r:
            self.readers.setdefault(k, {})[rk] = it
        for k in w:
            self.lastw[k] = it
            self.readers[k] = {}

    def op(self, eng, fn, r=(), w=(), extra=()):
        it = Item(eng, fn, "op")
        self._deps(it, r, w, extra)
        self.q[eng].append(it)
        return it

    def dma(self, eng, sem, out, in_, r=(), w=(), extra=()):
        it = Item(eng, lambda e: e.dma_start(out=out, in_=in_), "dma")
        it.dsem = sem
        self.dcnt[sem] = self.dcnt.get(sem, 0) + 16
        it.dval = self.dcnt[sem]
        self._deps(it, r, w, extra)
        self.q[eng].append(it)
        return it

    def finalize(self):
        for e in self.ENG:
            for it in self.q[e]:
                for d in it.deps:
                    if d.kind == "op" and not (d.eng == "pe" and it.eng == "pe"):
                        d.marked = True
        for e in self.ENG:
            c = 0
            for it in self.q[e]:
                if it.kind == "op" and it.marked:
                    c += 1
                    it.count = c
        nw = 0
        for e in self.ENG:
            seen = {}
            for it in self.q[e]:
                ws = {}
                for d in it.deps:
                    if d.kind == "dma":
                        s, v = d.dsem, d.dval
                    else:
                        if d.eng == "pe" and it.eng == "pe":
                            continue
                        s, v = d.eng, d.count
                    if seen.get(s, 0) >= v:
                        continue
                    if ws.get(s, 0) < v:
                        ws[s] = v
                for s, v in ws.items():
                    seen[s] = v
                it.waits = list(ws.items())
                nw += len(it.waits)
        return nw


def build(S, n_tiles=None, dbg=False):
    NT = S // T if n_tiles is None else n_tiles
    NKT = S // 128
    nc = bass.Bass("TRN2", target_bir_lowering=False)

    def dt(name, shape, d=F32, kind="ExternalInput"):
        return nc.dram_tensor(name, shape, d, kind=kind).ap()

    x = dt("x", [S, D])
    out = dt("out", [S, D], kind="ExternalOutput")
    W = {}
    for f in (1, 2):
        W[("g", f)] = dt(f"ffn{f}_w_gate", [D, DFF])
        W[("u", f)] = dt(f"ffn{f}_w_up", [D, DFF])
        W[("d", f)] = dt(f"ffn{f}_w_down", [DFF, D])
    w_in = dt("w_in", [D, 2048])
    w_out = dt("w_out", [D, D])
    pool_w = dt("pool_w", [4, 128, 128])
    gains_d = dt("gains", [128, 24])
    gfin_d = dt("gfin", [128, D])
    lamv_d = dt("lamv", [128, 256])
    sgain_d = dt("sgain", [128, 128])
    pscale_d = dt("pscale", [128, 4])
    cosT = dt("cosT", [128, S])
    sinT = dt("sinT", [128, S])
    ident_d = dt("ident", [128, 128])
    maskm_d = dt("maskm", [128, 896], BF16)
    rcnt_d = dt("rcnt", [128, 16])
    dbg_t = {}
    if dbg:
        for nm in ("x1", "x2", "x3"):
            dbg_t[nm] = dt("dbg_" + nm, [S // T, 128, 4, D], kind="ExternalOutput")
        dbg_t["catT"] = dt("dbg_catT", [S // T, 128, 8, T], BF16, kind="ExternalOutput")
        dbg_t["KT"] = dt("dbg_KT", [128, 4, S], BF16, kind="ExternalOutput")
        dbg_t["dT"] = dt("dbg_dT", [S // T, 128, 4, T], BF16, kind="ExternalOutput")

    chunks = []
    CI = {}

    def add(key, **kw):
        CI[key] = len(chunks)
        chunks.append(kw)

    def add_ffn(f):
        for c in range(NCF):
            add(("g", f, c), kind="cols", w=W[("g", f)], c0=(c // 2) * 256, grp=("g", f, c // 2), sub=c % 2, gw=256)
            add(("u", f, c), kind="cols", w=W[("u", f)], c0=(c // 2) * 256, grp=("u", f, c // 2), sub=c % 2, gw=256)
        for c in range(NCF):
            add(("d", f, c), kind="rows", w=W[("d", f)], r0=c * 128, grp=("d", f, c), sub=0)

    add_ffn(1)
    for g in range(4):
        add(("U", g), kind="cols", w=w_in, c0=1536 + (g // 2) * 256, grp=("U", g // 2), sub=g % 2, gw=256)
    for v in range(4):
        add(("V", v), kind="v", v=v, grp=("V", v), sub=0)
    for h in range(4):
        add(("Q", h), kind="cols", w=w_in, c0=(h // 2) * 256, grp=("Q", h // 2), sub=h % 2, gw=256)
        add(("QS", h), kind="cols", w=w_in, c0=(h // 2) * 256, grp=("Q", h // 2), sub=h % 2, gw=256, swap=True)
        add(("K", h), kind="cols", w=w_in, c0=512 + (h // 2) * 256, grp=("K", h // 2), sub=h % 2, gw=256)
        add(("KS", h), kind="cols", w=w_in, c0=512 + (h // 2) * 256, grp=("K", h // 2), sub=h % 2, gw=256, swap=True)
    for kc in range(8):
        add(("WO", kc), kind="rows", w=w_out, r0=kc * 128, grp=("WO", kc), sub=0)
    add_ffn(2)
    NPER = len(chunks)
    add("PW", kind="pw", grp=("PW",), sub=0)
    NCH = len(chunks)
    wbf = dt("wbf", [NCH, 128, 1024], BF16, kind="Internal")

    P = Prog()
    es = contextlib.ExitStack()
    with es:
        def sb(name, shape, d):
            return es.enter_context(nc.sbuf_tensor(name, shape, d))

        KT = sb("KT", [128, 4, S], BF16)
        VC = sb("VC", [128, NKT, 4, 130], BF16)
        xres = sb("xres", [128, 4, D], F32)
        xnT = sb("xnT", [128, 8, T], BF16)
        ring = sb("ring", [128, NSLOT, 1024], BF16)
        arena = sb("arena", [128, ARENA_KIB * 256], F32)
        rstd_b = sb("rstd_b", [128, T], F32)
        ident = sb("ident_s", [128, 128], F32)
        identb = sb("identb", [128, 128], BF16)
        zerosb = sb("zerosb", [128, 128], BF16)
        ones32 = sb("ones32", [128, 128], F32)
        maskm = sb("maskm_s", [128, 896], BF16)
        sgain08 = sb("sgain08", [128, 128], F32)
        pwb = sb("pwb", [128, 512], BF16)
        gains = sb("gains_s", [128, 24], F32)
        pscale = sb("pscale_s", [128, 4], F32)
        rcnt = sb("rcnt_s", [128, 16], F32)
        halo = sb("halo", [128, 4, 16], F32)
        small = sb("small", [128, 64], F32)
        junkS = sb("junkS", [128, 128], BF16)
        tmpf = sb("tmpf", [128, 16], F32)
        psb = [es.enter_context(nc.psum_tensor(f"ps{b}", [128, 512], F32)) for b in range(8)]

        ssq = small[:, 0:4]
        mse = small[:, 4:8]
        rstd = small[:, 8:12]
        neghalf = small[:, 12:20]
        rr = small[:, 20:28]
        rn = small[:, 28:32]
        ssa = small[:, 32:36]
        msa = small[:, 36:40]
        rsa = small[:, 40:44]
        lsc = small[:, 44:52]
        neglam = small[:, 49:50]

        def av(off, nbytes, d):
            a = arena[:, off // 4:(off + nbytes) // 4]
            return a if d == F32 else a.bitcast(d)

        def blk(off, nbytes):
            return [("A", i) for i in range(off // 1024, (off + nbytes + 1023) // 1024)]

        KB = 1024
        gT = av(0, 22 * KB, BF16).rearrange("p (c n) -> p c n", n=T)
        sg = [av(22 * KB + i * 2 * KB, 2 * KB, F32) for i in range(2)]
        stage = av(0, 16 * KB, F32).rearrange("p (s d) -> p s d", d=D)
        R0 = av(27 * KB, 2 * KB, F32)
        R1 = av(29 * KB, 2 * KB, F32)
        gfinb = av(27 * KB, 4 * KB, F32)
        kR0, kR1 = blk(27 * KB, 2 * KB), blk(29 * KB, 2 * KB)
        QT = av(0, 8 * KB, BF16).rearrange("p (h m n) -> p h m n", h=4, m=2)
        poolT = av(8 * KB, 4 * KB, BF16).rearrange("p (g n) -> p g n", n=T)
        dT = av(12 * KB, 4 * KB, BF16).rearrange("p (g n) -> p g n", n=T)
        PT = av(16 * KB, NPT * KB, BF16).rearrange("p (b n) -> p b n", n=T)
        DmS = [av((16 + s_) * KB, 512, F32) for s_ in range(4)]
        UB = 2112
        uoff = [20 * KB, 20 * KB + UB, 20 * KB + 2 * UB]
        ubuf = [av(o, UB, F32) for o in uoff]
        kub = [blk(o, UB) for o in uoff]
        aB = R0.rearrange("p (q n) -> p q n", n=128)
        aA = R1[:, 0:256].rearrange("p (q n) -> p q n", n=128)
        an = R1[:, 256:512].bitcast(BF16).rearrange("p (q n) -> p q n", n=128)
        st32 = [av(b * 4 * KB, 4 * KB, F32) for b in range(4)]
        st16 = [av(16 * KB + b * 2 * KB, 2 * KB, BF16) for b in range(4)]
        junk_sb = av(20 * KB, 2 * KB, BF16)
        kjunk = blk(20 * KB, 2 * KB)
        ps3b = psb[3][:].bitcast(BF16)

        def kps(b):
            return ("ps", b)

        setup_items = []
        for (dst, src, key) in [
            (ident[:], ident_d, "ident"), (maskm[:], maskm_d, "maskm"), (gains[:], gains_d, "gains"),
            (pscale[:], pscale_d, "pscale"), (rcnt[:], rcnt_d, "rcnt"),
            (sgain08[:], sgain_d, "sgain08"), (R0[:, 0:256], lamv_d, "lamv"),
        ]:
            setup_items.append(P.dma("sp", "setup", out=dst, in_=src, w=[key] if key != "lamv" else kR0))
        fence = P.op("sp", lambda e: e.nop(), extra=setup_items)
        for key in ["ident", "maskm", "gains", "pscale", "rcnt", "sgain08"] + kR0:
            P.lastw[key] = setup_items[-1]
        P.op("dve", lambda e: e.memset(zerosb[:], 0.0), w=["zerosb"])
        P.op("dve", lambda e: e.memset(ones32[:], 1.0), w=["ones32"])
        P.op("dve", lambda e: e.memset(neghalf, -0.5), w=["neghalf"])
        P.op("dve", lambda e: e.memset(halo[:], 0.0), w=["halo"])
        P.op("pool", lambda e: e.memset(QT[:, :, :, :], 0.0), w=blk(0, 8 * KB))
        P.op("pool", lambda e: e.memset(VC[:, :, :, 128:129], 1.0), w=["VCinit"])
        P.op("pool", lambda e: e.memset(VC[:, :, :, 129:130], 0.0), w=["VCinit"])
        P.op("dve", lambda e: e.tensor_copy(out=identb[:], in_=ident[:]), r=["ident"], w=["identb"])
        P.op("dve", lambda e: e.tensor_scalar(out=sgain08[:], in0=sgain08[:], scalar1=0.8, scalar2=None,
                                              op0=ALU.mult), r=["sgain08"], w=["sgain08"])
        lam_in = R0[:, 0:256]
        lam_t = R0[:, 256:384]
        P.op("dve", lambda e: e.tensor_tensor(out=lam_t[:, 0:64], in0=lam_in[:, 0:64], in1=lam_in[:, 64:128],
                                              op=ALU.mult), r=kR0, w=kR0)
        P.op("dve", lambda e: e.tensor_tensor(out=lam_t[:, 64:128], in0=lam_in[:, 128:192], in1=lam_in[:, 192:256],
                                              op=ALU.mult), r=kR0, w=kR0)
        P.op("act", lambda e: e.activation(out=junkS[:, 0:64], in_=lam_t[:, 0:64], func=AF.Copy,
                                           accum_out=lsc[:, 0:1]), r=kR0, w=["junkS", "l1"])
        P.op("act", lambda e: e.activation(out=junkS[:, 64:128], in_=lam_t[:, 64:128], func=AF.Copy,
                                           accum_out=lsc[:, 1:2]), r=kR0, w=["junkS", "l2"])
        P.op("act", lambda e: e.activation(out=lsc[:, 2:4], in_=lsc[:, 0:2], func=AF.Exp), r=["l1", "l2"], w=["e12"])
        P.op("dve", lambda e: e.tensor_tensor(out=lsc[:, 4:5], in0=lsc[:, 3:4], in1=lsc[:, 2:3], op=ALU.subtract),
             r=["e12"], w=["lamd"])
        P.op("dve", lambda e: e.tensor_scalar(out=neglam, in0=lsc[:, 4:5], scalar1=-0.2, scalar2=None, op0=ALU.add),
             r=["lamd"], w=["neglam"])

        NB32, NB16 = 4, 12
        if S >= 8192:
            def ktreg(h, ti, ntl, d):
                a_ = KT[:, h, ti * 512:(ti + ntl) * 512]
                return (a_ if d == BF16 else a_.bitcast(d)), [("KT", h, tt) for tt in range(ti, ti + ntl)]
            st32r = [ktreg(h, 1, 8, F32) for h in range(4)]
            st16r = [ktreg(h, 9 + 2 * i, 2, BF16) for h in range(4) for i in range(3)]
        else:
            stg32 = sb("stg32", [128, NB32, 2048], F32)
            stg16 = sb("stg16", [128, NB16, 1024], BF16)
            st32r = [(stg32[:, b, :], [("stg32", b)]) for b in range(NB32)]
            st16r = [(stg16[:, b, :], [("stg16", b)]) for b in range(NB16)]

        grp_buf = {}
        grp_cnt = [0]

        def convert_chunk(ci, dst16, kdst16, ceng):
            ch = chunks[ci]
            kind = ch["kind"]
            gid = ch["grp"]
            if gid not in grp_buf:
                b32 = grp_cnt[0] % NB32
                grp_cnt[0] += 1
                s32, k32 = st32r[b32]
                if kind == "cols":
                    gw = ch["gw"]
                    src = ch["w"].rearrange("(kc p) n -> p kc n", p=128)[:, :, ch["c0"]:ch["c0"] + gw]
                    dst = s32[:, 0:8 * gw].rearrange("p (kc n) -> p kc n", n=gw)
                elif kind == "rows":
                    src = ch["w"][ch["r0"]:ch["r0"] + 128, :]
                    dst = s32[:, 0:1024]
                elif kind == "v":
                    v = ch["v"]
                    src = w_in.rearrange("(kc p) n -> p kc n", p=128)[:, 2 * v:2 * v + 2, 1024:1536]
                    dst = s32[:, 0:1024].rearrange("p (kc n) -> p kc n", n=512)
                else:
                    src = pool_w.rearrange("g p d -> p g d")
                    dst = s32[:, 0:512].rearrange("p (g d) -> p g d", d=128)
                P.dma("sp", f"pl{b32}", out=dst, in_=src, w=k32)
                grp_buf[gid] = (s32, k32)
            s32, k32 = grp_buf[gid]

            def cp(o, i):
                if ceng == "act":
                    P.op("act", lambda e: e.activation(out=o, in_=i, func=AF.Copy), r=k32, w=kdst16)
                else:
                    P.op(ceng, lambda e: e.tensor_copy(out=o, in_=i), r=k32, w=kdst16)

            if kind == "cols":
                gw = ch["gw"]
                sub = ch["sub"]
                iv = s32[:, 0:8 * gw].rearrange("p (kc n) -> p kc n", n=gw)[:, :, sub * 128:(sub + 1) * 128]
                ov = dst16.rearrange("p (kc n) -> p kc n", n=128)
                if ch.get("swap"):
                    i4 = iv.rearrange("p kc (m d) -> p kc m d", d=64)
                    o4 = ov.rearrange("p kc (m d) -> p kc m d", d=64)
                    cp(o4[:, :, :, 0:32], i4[:, :, :, 32:64])
                    cp(o4[:, :, :, 32:64], i4[:, :, :, 0:32])
                else:
                    cp(ov, iv)
            elif kind == "pw":
                cp(dst16[:, 0:512], s32[:, 0:512])
            else:
                cp(dst16[:, 0:1024], s32[:, 0:1024])

        conv_n = [0]
        convert_chunk(CI["PW"], pwb[:], ["pwb"], "pool")

        seq = [ci for _ in range(NT) for ci in range(NPER)]
        st = dict(issued=0, consumed=0, nxt=0)
        chunk_ap = {}
        pend_st = []
        STORE_LAG = 5

        def prefetch():
            while st["issued"] < len(seq):
                n = st["issued"]
                depth = NB16 if n < NPER else NSLOT
                if n >= st["consumed"] + depth:
                    break
                ci = seq[n]
                if n < NPER:
                    ap16, k16 = st16r[n % NB16]
                    convert_chunk(ci, ap16, k16, ("pool", "act", "dve")[n % 3])
                    pend_st.append((n, ci, ap16, k16))
                    while pend_st and (len(pend_st) > STORE_LAG or n == NPER - 1):
                        n2, ci2, a2, k2 = pend_st.pop(0)
                        P.dma("sp", f"pst{n2 % NB16}", out=wbf[ci2, :, :], in_=a2, r=k2, w=[("wbf", ci2)])
                    chunk_ap[n] = (ap16, k16)
                else:
                    slot = n % NSLOT
                    P.dma("sp", f"slot{slot}", out=ring[:, slot, :], in_=wbf[ci, :, :],
                          r=[("wbf", ci)], w=[("slot", slot)])
                    chunk_ap[n] = (ring[:, slot, :], [("slot", slot)])
                st["issued"] += 1

        def need(key):
            n = st["nxt"]
            assert seq[n] == CI[key], (key, n, seq[n], CI[key])
            st["nxt"] += 1
            if n >= st["issued"]:
                prefetch()
            assert n < st["issued"]
            return chunk_ap.pop(n)

        def done(k=1):
            st["consumed"] += k
            prefetch()

        def mm(o, lhsT, rhs, start, stop, r, w, **kw):
            return P.op("pe", lambda e: e.matmul(o, lhsT, rhs, start=start, stop=stop, **kw), r=r, w=w)

        def rms_stats(n_out=4):
            for s in range(4):
                P.op("act", lambda e, s=s: e.activation(out=junk_sb, in_=xres[:, s, :], func=AF.Square,
                                                        accum_out=ssq[:, s:s + 1]),
                     r=[("xres", s)], w=kjunk + [("ssq", s)])
            P.op("dve", lambda e: e.tensor_scalar(out=mse, in0=ssq, scalar1=1.0 / D, scalar2=EPS,
                                                  op0=ALU.mult, op1=ALU.add),
                 r=[("ssq", s) for s in range(4)], w=["mse"] + [("mse", s_) for s_ in range(4)])
            P.op("pool", lambda e: e.tensor_tensor(out=rstd, in0=mse, in1=neghalf[:, 0:4], op=ALU.pow),
                 r=["mse", "neghalf"], w=["rstd"] + [("rstd", s_) for s_ in range(4)])

        def norm_to_xnT(gi):
            for s in range(4):
                P.op("act", lambda e, s=s: e.activation(out=junk_sb, in_=xres[:, s, :], func=AF.Square,
                                                        accum_out=ssq[:, s:s + 1]),
                     r=[("xres", s)], w=kjunk + [("ssq", s)])
                P.op("dve", lambda e, s=s: e.tensor_scalar(out=mse[:, s:s + 1], in0=ssq[:, s:s + 1],
                                                           scalar1=1.0 / D, scalar2=EPS, op0=ALU.mult, op1=ALU.add),
                     r=[("ssq", s)], w=[("mse", s)])
                P.op("pool", lambda e, s=s: e.tensor_tensor(out=rstd[:, s:s + 1], in0=mse[:, s:s + 1],
                                                            in1=neghalf[:, 0:1], op=ALU.pow),
                     r=[("mse", s), "neghalf"], w=[("rstd", s)])
                P.op("dve", lambda e, s=s: e.tensor_scalar(out=DmS[s], in0=ident[:], scalar1=rstd[:, s:s + 1],
                                                           scalar2=None, op0=ALU.mult),
                     r=["ident", ("rstd", s)], w=[("A", 16 + s)])
                for kc in range(8):
                    mm(psb[kc][:, s * 128:(s + 1) * 128], xres[:, s, kc * 128:(kc + 1) * 128], DmS[s], True, True,
                       r=[("xres", s), ("A", 16 + s)], w=[kps(kc)])
            for kc in range(8):
                gcol = gains[:, gi * 8 + kc:gi * 8 + kc + 1]
                if kc % 2 == 0:
                    P.op("act", lambda e, kc=kc, gcol=gcol: e.activation(out=xnT[:, kc, :], in_=psb[kc][:],
                                                                         func=AF.Copy, scale=gcol),
                         r=[kps(kc), "gains"], w=[("xnT", kc)])
                else:
                    P.op("dve", lambda e, kc=kc, gcol=gcol: e.tensor_scalar(out=xnT[:, kc, :], in0=psb[kc][:],
                                                                            scalar1=gcol, scalar2=None, op0=ALU.mult),
                         r=[kps(kc), "gains"], w=[("xnT", kc)])

        def ffn(f, gi):
            norm_to_xnT(gi)
            for c in range(NCF):
                par = c % 2
                for (nm, bank) in (("g", par), ("u", 2 + par)):
                    wap, wk = need((nm, f, c))
                    for kc in range(8):
                        mm(psb[bank][:], wap[:, kc * 128:(kc + 1) * 128], xnT[:, kc, :], kc == 0, kc == 7,
                           r=wk + [("xnT", kc)], w=[kps(bank)])
                    done()
                ksg = blk(22 * KB + par * 2 * KB, 2 * KB)
                P.op("act", lambda e, par=par: e.activation(out=sg[par], in_=psb[par][:], func=AF.Silu),
                     r=[kps(par)], w=ksg)
                P.op("dve", lambda e, par=par, c=c: e.tensor_tensor(out=gT[:, c, :], in0=psb[2 + par][:], in1=sg[par],
                                                                    op=ALU.mult),
                     r=[kps(2 + par)] + ksg, w=[("A", c)])
            for c in range(NCF):
                wap, wk = need(("d", f, c))
                for s in range(4):
                    for hf in range(2):
                        b = s * 2 + hf
                        mm(psb[b][:], gT[:, c, s * 128:(s + 1) * 128], wap[:, hf * 512:(hf + 1) * 512],
                           c == 0, c == NCF - 1, r=wk + [("A", c)], w=[kps(b)])
                done()
            for s in range(4):
                for hf in range(2):
                    b = s * 2 + hf
                    xs = xres[:, s, hf * 512:(hf + 1) * 512]
                    P.op("dve", lambda e, b=b, xs=xs: e.scalar_tensor_tensor(
                        out=xs, in0=psb[b][:], scalar=0.5, in1=xs, op0=ALU.mult, op1=ALU.add),
                         r=[kps(b), ("xres", s)], w=[("xres", s)])

        cnt = [0]

        def attention_tile(t):
            nk = 4 * (t + 1)
            per_head = 2 * nk
            units = [(h, j, m) for h in range(4) for j in range(nk) for m in range(2)]
            n = len(units)
            LA = 2
            info = {}

            def kQ(h, m):
                return ("A", 2 * h + m)

            def qk(u):
                h, j, m = units[u]
                diag = j >= 4 * t
                jj = j - 4 * t
                sbk = 4 + cnt[0] % 3
                pb = cnt[0] % NPT
                cnt[0] += 1
                info[u] = (sbk, pb)
                mm(psb[sbk][:], KT[:, h, j * 128:(j + 1) * 128], QT[:, h, m, :], True, not diag,
                   r=[("KT", h, j // 4), kQ(h, m)], w=[kps(sbk)])
                if diag:
                    mm(psb[sbk][:], identb[:], maskm[:, 384 - 128 * jj:896 - 128 * jj], False, True,
                       r=["identb", "maskm"], w=[kps(sbk)])

            def ex(u):
                sbk, pb = info[u]
                P.op("act", lambda e, sbk=sbk, pb=pb: e.activation(out=PT[:, pb, :], in_=psb[sbk][:],
                                                                   func=AF.Exp, scale=0.125),
                     r=[kps(sbk)], w=[("A", 16 + pb)])

            def av(u):
                h, j, m = units[u]
                sbk, pb = info[u]
                diag = j >= 4 * t
                jj = j - 4 * t
                for qs in range(jj if diag else 0, 4):
                    a = qs * 2 + m
                    bank, off = a // 3, (a % 3) * 130
                    last = (j == 4 * t + qs)
                    mm(psb[bank][:, off:off + 130], PT[:, pb, qs * 128:(qs + 1) * 128], VC[:, j, h, :],
                       False, last, r=[("A", 16 + pb), ("VC", j), "VCinit"], w=[kps(bank)], skip_group_check=True)

            def zero_O():
                for b in range(3):
                    mm(psb[b][:], zerosb[:], maskm[:, 0:512], True, False, r=["zerosb", "maskm"], w=[kps(b)],
                       skip_group_check=True)

            def epiA(h):
                for qs in range(4):
                    a1, a2 = qs * 2, qs * 2 + 1
                    b1, o1 = a1 // 3, (a1 % 3) * 130
                    b2, o2 = a2 // 3, (a2 % 3) * 130
                    P.op("dve", lambda e, qs=qs, b1=b1, o1=o1: e.reciprocal(out=rr[:, 2 * qs:2 * qs + 1],
                                                                            in_=psb[b1][:, o1 + 128:o1 + 129]),
                         r=[kps(b1)], w=[("rr", 2 * qs)])
                    P.op("dve", lambda e, qs=qs, b2=b2, o2=o2: e.reciprocal(out=rr[:, 2 * qs + 1:2 * qs + 2],
                                                                            in_=psb[b2][:, o2 + 128:o2 + 129]),
                         r=[kps(b2)], w=[("rr", 2 * qs + 1)])
                    P.op("dve", lambda e, qs=qs: e.tensor_scalar(out=rn[:, qs:qs + 1],
                                                                 in0=rr[:, 2 * qs + 1:2 * qs + 2],
                                                                 scalar1=neglam, scalar2=None, op0=ALU.mult),
                         r=[("rr", 2 * qs + 1), "neglam"], w=[("rn", qs)])
                    P.op("act", lambda e, qs=qs, b1=b1, o1=o1: e.activation(out=aA[:, qs % 2, :],
                                                                            in_=psb[b1][:, o1:o1 + 128],
                                                                            func=AF.Copy,
                                                                            scale=rr[:, 2 * qs:2 * qs + 1]),
                         r=[kps(b1), ("rr", 2 * qs)], w=[("aA", qs % 2)] + (kR1 if (h == 0 and qs == 0) else []))
                    P.op("dve", lambda e, qs=qs, b2=b2, o2=o2: e.scalar_tensor_tensor(
                        out=aB[:, qs, :], in0=psb[b2][:, o2:o2 + 128], scalar=rn[:, qs:qs + 1],
                        in1=aA[:, qs % 2, :], op0=ALU.mult, op1=ALU.add),
                         r=[kps(b2), ("rn", qs), ("aA", qs % 2)], w=[("aB", qs)] + (kR0 if (h == 0 and qs == 0) else []))
                    P.op("act", lambda e, qs=qs: e.activation(out=junkS[:], in_=aB[:, qs, :], func=AF.Square,
                                                              accum_out=ssa[:, qs:qs + 1]),
                         r=[("aB", qs)], w=["junkS", ("ssa", qs)])
                P.op("dve", lambda e: e.tensor_scalar(out=msa, in0=ssa, scalar1=1.0 / 128, scalar2=EPS,
                                                      op0=ALU.mult, op1=ALU.add),
                     r=[("ssa", q) for q in range(4)], w=["msa"])
                P.op("pool", lambda e: e.tensor_tensor(out=rsa, in0=msa, in1=neghalf[:, 0:4], op=ALU.pow),
                     r=["msa", "neghalf"], w=["rsa"])
                for qs in range(4):
                    P.op("dve", lambda e, qs=qs: e.scalar_tensor_tensor(
                        out=an[:, qs, :], in0=aB[:, qs, :], scalar=rsa[:, qs:qs + 1], in1=sgain08[:],
                        op0=ALU.mult, op1=ALU.mult),
                         r=[("aB", qs), "rsa", "sgain08"], w=[("an", qs)])

            def epiB(h):
                for qs in range(4):
                    P.op("pe", lambda e, qs=qs: e.transpose(out=ps3b[:, qs * 128:(qs + 1) * 128], in_=an[:, qs, :],
                                                            identity=identb[:]),
                         r=[("an", qs), "identb"], w=[kps(3)])
                P.op("act", lambda e: e.activation(out=QT[:, h, 0, :], in_=ps3b[:, 0:512], func=AF.Copy),
                     r=[kps(3)], w=[kQ(h, 0)])

            pending = [None]
            for u in range(min(LA, n)):
                qk(u)
            for u in range(n):
                h, j, m = units[u]
                pos = u % per_head
                if pos == 0:
                    zero_O()
                if u + LA < n:
                    qk(u + LA)
                ex(u)
                av(u)
                if pending[0] is not None and pos == min(7, per_head - 1):
                    epiB(pending[0])
                    pending[0] = None
                if pos == per_head - 1:
                    epiA(h)
                    pending[0] = h
            if pending[0] is not None:
                epiB(pending[0])

        def mixer(t):
            norm_to_xnT(1)
            cosb, sinb = R0, R1
            P.dma("sp", "cos", out=cosb, in_=cosT[:, t * T:(t + 1) * T], w=kR0)
            P.dma("sp", "sin", out=sinb, in_=sinT[:, t * T:(t + 1) * T], w=kR1)
            P.op("pool", lambda e: e.memset(QT[64:128, :, 0, :], 0.0), w=[("A", 2 * h) for h in range(4)])
            P.op("pool", lambda e: e.memset(QT[0:64, :, 1, :], 0.0), w=[("A", 2 * h + 1) for h in range(4)])
            uA, uB, uC = ubuf
            for g in range(4):
                wap, wk = need(("U", g))
                for kc in range(8):
                    mm(psb[4 + g][:], wap[:, kc * 128:(kc + 1) * 128], xnT[:, kc, :], kc == 0, kc == 7,
                       r=wk + [("xnT", kc)], w=[kps(4 + g)])
                done()
                P.op("act", lambda e, g=g: e.activation(out=uA[:, 16:528], in_=psb[4 + g][:], func=AF.Copy),
                     r=[kps(4 + g)], w=kub[0])
                P.op("pool", lambda e, g=g: e.tensor_copy(out=uA[:, 0:16], in_=halo[:, g, :]),
                     r=[("halo", g), "halo"], w=kub[0])
                P.op("pool", lambda e, g=g: e.tensor_copy(out=halo[:, g, :], in_=uA[:, 512:528]),
                     r=kub[0], w=[("halo", g)])
                cur, kcur = uA, kub[0]
                for lvl in range(g + 1):
                    sh = 1 << lvl
                    lo = 2 * sh - 1
                    dst, kdst = (uB, kub[1]) if lvl % 2 == 0 else (uC, kub[2])
                    P.op("pool", lambda e, cur=cur, dst=dst, lo=lo, sh=sh: e.tensor_tensor(
                        out=dst[:, lo:528], in0=cur[:, lo:528], in1=cur[:, lo - sh:528 - sh], op=ALU.add),
                         r=kcur, w=kdst)
                    cur, kcur = dst, kdst
                win = 2 << g
                P.op("dve", lambda e, g=g, cur=cur, win=win: e.scalar_tensor_tensor(
                    out=dT[:, g, :], in0=cur[:, 16:528], scalar=1.0 / win, in1=uA[:, 16:528],
                    op0=ALU.mult, op1=ALU.subtract), r=kcur + kub[0], w=[("A", 12 + g)])
                if t == 0:
                    n = win - 1
                    P.op("dve", lambda e, cur=cur, n=n: e.tensor_tensor(out=tmpf[:, 0:n], in0=cur[:, 16:16 + n],
                                                                        in1=rcnt[:, 0:n], op=ALU.mult),
                         r=kcur + ["rcnt"], w=["tmpf"])
                    P.op("dve", lambda e, g=g, n=n: e.tensor_tensor(out=dT[:, g, 0:n], in0=tmpf[:, 0:n],
                                                                    in1=uA[:, 16:16 + n], op=ALU.subtract),
                         r=["tmpf"] + kub[0], w=[("A", 12 + g)])
            vsl = [need(("V", v)) for v in range(4)]
            for v in range(4):
                for kk in range(2):
                    kc = 2 * v + kk
                    for s in range(4):
                        mm(psb[s][:], xnT[:, kc, s * 128:(s + 1) * 128], vsl[v][0][:, kk * 512:(kk + 1) * 512],
                           kc == 0, kc == 7, r=vsl[v][1] + [("xnT", kc)], w=[kps(s)])
                done()
            for s in range(4):
                P.op("act", lambda e, s=s: e.activation(out=VC[:, 4 * t + s, :, 0:128],
                                                        in_=psb[s][:].rearrange("p (h d) -> p h d", h=4),
                                                        func=AF.Copy),
                     r=[kps(s)], w=[("VC", 4 * t + s)])
            for h in range(4):
                bo = 4 if h % 2 == 0 else 0
                for (nm, bank) in (("Q", bo), ("QS", bo + 1), ("K", bo + 2), ("KS", bo + 3)):
                    wap, wk = need((nm, h))
                    for kc in range(8):
                        mm(psb[bank][:], wap[:, kc * 128:(kc + 1) * 128], xnT[:, kc, :], kc == 0, kc == 7,
                           r=wk + [("xnT", kc)], w=[kps(bank)])
                    done()
                for (b0, isq) in ((bo, True), (bo + 2, False)):
                    P.op("dve", lambda e, b0=b0: e.tensor_tensor(out=psb[b0][:], in0=psb[b0][:], in1=cosb,
                                                                 op=ALU.mult), r=[kps(b0)] + kR0, w=[kps(b0)])
                    P.op("dve", lambda e, b0=b0: e.tensor_tensor(out=rstd_b[:], in0=psb[b0 + 1][:], in1=sinb,
                                                                 op=ALU.mult), r=[kps(b0 + 1)] + kR1, w=["rstd_b"])
                    if isq:
                        P.op("dve", lambda e, h=h, b0=b0: e.tensor_tensor(out=QT[0:64, h, 0, :], in0=psb[b0][0:64, :],
                                                                          in1=rstd_b[0:64, :], op=ALU.add),
                             r=[kps(b0), "rstd_b"], w=[("A", 2 * h)])
                        P.op("dve", lambda e, h=h, b0=b0: e.tensor_tensor(out=QT[64:128, h, 1, :],
                                                                          in0=psb[b0][64:128, :],
                                                                          in1=rstd_b[64:128, :], op=ALU.add),
                             r=[kps(b0), "rstd_b"], w=[("A", 2 * h + 1)])
                    else:
                        P.op("dve", lambda e, h=h, b0=b0: e.tensor_tensor(out=KT[:, h, t * T:(t + 1) * T],
                                                                          in0=psb[b0][:], in1=rstd_b[:], op=ALU.add),
                             r=[kps(b0), "rstd_b"], w=[("KT", h, t)])
            attention_tile(t)
            for g in range(4):
                bank = 4 + g % 2
                mm(psb[bank][:], pwb[:, g * 128:(g + 1) * 128], dT[:, g, :], True, True,
                   r=["pwb", ("A", 12 + g)], w=[kps(bank)])
                P.op("act", lambda e, g=g, bank=bank: e.activation(out=poolT[:, g, :], in_=psb[bank][:], func=AF.Copy,
                                                                   scale=pscale[:, g:g + 1]),
                     r=[kps(bank), "pscale"], w=[("A", 8 + g)])
            if dbg:
                P.dma("sp", "dbg", out=dbg_t["catT"][t][:, 0:4, :], in_=QT[:, :, 0, :], r=[("A", i) for i in range(8)])
                P.dma("sp", "dbg", out=dbg_t["catT"][t][:, 4:8, :], in_=poolT[:, :, :], r=[("A", 8 + i) for i in range(4)])
                P.dma("sp", "dbg", out=dbg_t["dT"][t], in_=dT[:, :, :], r=[("A", 12 + i) for i in range(4)])
            for kc in range(8):
                wap, wk = need(("WO", kc))
                lhs = QT[:, kc, 0, :] if kc < 4 else poolT[:, kc - 4, :]
                kl = ("A", 2 * kc) if kc < 4 else ("A", 8 + kc - 4)
                for s in range(4):
                    for hf in range(2):
                        b = s * 2 + hf
                        mm(psb[b][:], lhs[:, s * 128:(s + 1) * 128], wap[:, hf * 512:(hf + 1) * 512],
                           kc == 0, kc == 7, r=wk + [kl], w=[kps(b)])
                done()
            for s in range(4):
                for hf in range(2):
                    b = s * 2 + hf
                    xs = xres[:, s, hf * 512:(hf + 1) * 512]
                    P.op("dve", lambda e, b=b, xs=xs: e.tensor_tensor(out=xs, in0=psb[b][:], in1=xs, op=ALU.add),
                         r=[kps(b), ("xres", s)], w=[("xres", s)])

        last_store = [None]

        def final(t):
            P.dma("sp", "gfin", out=gfinb, in_=gfin_d,
                  w=kR0 + kR1 + [("aA", 0), ("aA", 1)] + [("aB", q) for q in range(4)] + [("an", q) for q in range(4)])
            rms_stats()
            for s in range(4):
                P.op("dve", lambda e, s=s: e.scalar_tensor_tensor(
                    out=stage[:, s, :], in0=xres[:, s, :], scalar=rstd[:, s:s + 1], in1=gfinb,
                    op0=ALU.mult, op1=ALU.mult),
                     r=[("xres", s), "rstd"] + kR0 + kR1, w=blk(s * 4 * KB, 4 * KB))
            last_store[0] = P.dma("act", "ost",
                                  out=out[t * T:(t + 1) * T, :].rearrange("(s p) d -> p s d", p=128),
                                  in_=stage, r=blk(0, 16 * KB))

        for t in range(NT):
            for s in range(4):
                P.dma("sp", f"x{s}", out=xres[:, s, :], in_=x[t * T + s * 128:t * T + (s + 1) * 128, :],
                      w=[("xres", s)])
            ffn(1, 0)
            if dbg:
                P.dma("sp", "dbg", out=dbg_t["x1"][t], in_=xres[:, :, :], r=[("xres", s) for s in range(4)])
            mixer(t)
            if dbg:
                P.dma("sp", "dbg", out=dbg_t["x2"][t], in_=xres[:, :, :], r=[("xres", s) for s in range(4)])
            ffn(2, 2)
            if dbg:
                P.dma("sp", "dbg", out=dbg_t["x3"][t], in_=xres[:, :, :], r=[("xres", s) for s in range(4)])
            final(t)
        if dbg:
            last_store[0] = P.dma("sp", "dbg", out=dbg_t["KT"], in_=KT[:, :, :], r=[("KT", h, t) for h in range(4) for t in range(NT)])
        P.op("sp", lambda e: e.nop(), extra=[last_store[0]])
        P.op("act", lambda e: e.nop(), extra=[last_store[0]])

        nwaits = P.finalize()
        sem_names = set(P.ENG) | set(P.dcnt.keys())
        sems = {n: es.enter_context(nc.semaphore(n)) for n in sorted(sem_names)}

        def replay(name):
            def run(e):
                for it in P.q[name]:
                    for (sn, v) in it.waits:
                        e.wait_ge(sems[sn], v)
                    ins = it.fn(e)
                    if it.kind == "dma":
                        ins.then_inc(sems[it.dsem], 16)
                    elif it.marked:
                        ins.then_inc(sems[it.eng], 1)
            return run

        with nc.Block() as block:
            block.tensor(replay("pe"))
            block.scalar(replay("act"))
            block.vector(replay("dve"))
            block.gpsimd(replay("pool"))
            block.sync(replay("sp"))
    nc._stats = {e: len(P.q[e]) for e in P.ENG}
    nc._stats["waits"] = nwaits
    return nc


def host_consts(S):
    inv_freq = (1.0 / (np.float32(10000.0) ** (np.arange(0, 64, 2, dtype=np.float32) / np.float32(64)))).astype(np.float32)
    pos = np.arange(S, dtype=np.float32)
    ang = (pos[:, None] * inv_freq[None, :]).astype(np.float32)
    c = np.cos(ang).astype(np.float32).T
    s = np.sin(ang).astype(np.float32).T
    cos64 = np.concatenate([c, c], axis=0)
    sin64 = np.concatenate([-s, s], axis=0)
    cosT = np.ascontiguousarray(np.concatenate([cos64, cos64], axis=0))
    sinT = np.ascontiguousarray(np.concatenate([sin64, sin64], axis=0))
    p = np.arange(128)[:, None]
    cc = np.arange(896)[None, :]
    maskm = np.where(cc >= 384 + 64 * (p >= 64), 0.0, -30000.0).astype(ml_dtypes.bfloat16)
    rcnt = np.broadcast_to((1.0 / np.arange(1, 17, dtype=np.float32))[None, :], (128, 16)).copy()
    return dict(cosT=cosT, sinT=sinT, ident=np.eye(128, dtype=np.float32), maskm=maskm, rcnt=rcnt)


def host_params(inp):
    f32 = lambda a: np.ascontiguousarray(np.asarray(a, dtype=np.float32))
    g3 = [f32(inp[k])[0].reshape(8, 128).T for k in ("ffn1_norm", "mix_norm", "ffn2_norm")]
    m = dict(
        gains=np.ascontiguousarray(np.concatenate(g3, axis=1)),
        gfin=np.ascontiguousarray(np.broadcast_to(f32(inp["final_norm"])[None, :], (128, D))),
        lamv=np.ascontiguousarray(np.broadcast_to(np.concatenate(
            [f32(inp[k])[0] for k in ("lambda_q1", "lambda_k1", "lambda_q2", "lambda_k2")])[None, :], (128, 256))),
        sgain=np.ascontiguousarray(np.broadcast_to(f32(inp["subln_gain"])[0][None, :], (128, 128))),
        pscale=np.ascontiguousarray(f32(inp["pool_scale"])[0].reshape(4, 128).T),
        w_in=f32(inp["w_in"])[0], w_out=f32(inp["w_out"])[0], pool_w=f32(inp["pool_w"])[0],
    )
    for f in (1, 2):
        for nm in ("gate", "up", "down"):
            m[f"ffn{f}_w_{nm}"] = f32(inp[f"ffn{f}_w_{nm}"])[0]
    return m


def kernel(**inputs):
    x = np.asarray(inputs["x"], dtype=np.float32)
    B, S, _ = x.shape
    nc = build(S)
    shared = host_params(inputs)
    shared.update(host_consts(S))
    in_maps = []
    for b in range(B):
        m = dict(shared)
        m["x"] = np.ascontiguousarray(x[b])
        in_maps.append(m)
    res = run_bass_kernel_spmd(nc, in_maps, core_ids=list(range(B)))
    return np.stack([np.asarray(r["out"], dtype=np.float32) for r in res.results], axis=0)
```

```python
import contextlib
import numpy as np
import ml_dtypes
import concourse.bass as bass
import concourse.mybir as mybir
from concourse.bass_utils import run_bass_kernel_spmd

F32 = mybir.dt.float32
BF16 = mybir.dt.bfloat16
AF = mybir.ActivationFunctionType
ALU = mybir.AluOpType

D = 1024
DFF = 2816
NCF = 22
T = 512
NSLOT = 8
NPT = 4
EPS = 1e-6
ARENA_KIB = 31


class Item:
    __slots__ = ("eng", "fn", "kind", "deps", "marked", "count", "dsem", "dval", "waits")

    def __init__(self, eng, fn, kind):
        self.eng = eng
        self.fn = fn
        self.kind = kind
        self.deps = []
        self.marked = False
        self.count = None
        self.dsem = None
        self.dval = None
        self.waits = []


class Prog:
    ENG = ["pe", "act", "dve", "pool", "sp"]

    def __init__(self):
        self.q = {e: [] for e in self.ENG}
        self.lastw = {}
        self.readers = {}
        self.dcnt = {}

    def _deps(self, it, r, w, extra):
        deps = {}
        for k in r:
            lw = self.lastw.get(k)
            if lw is not None:
                deps[id(lw)] = lw
        for k in w:
            lw = self.lastw.get(k)
            if lw is not None:
                deps[id(lw)] = lw
            for rd in self.readers.get(k, {}).values():
                deps[id(rd)] = rd
        for x in extra:
            if x is not None:
                deps[id(x)] = x
        deps.pop(id(it), None)
        it.deps = list(deps.values())
        rk = it.dsem if it.kind == "dma" else it.eng
        for k in r:
            self.readers.setdefault(k, {})[rk] = it
        for k in w:
            self.lastw[k] = it
            self.readers[k] = {}

    def op(self, eng, fn, r=(), w=(), extra=()):
        it = Item(eng, fn, "op")
        self._deps(it, r, w, extra)
        self.q[eng].append(it)
        return it

    def dma(self, eng, sem, out, in_, r=(), w=(), extra=()):
        it = Item(eng, lambda e: e.dma_start(out=out, in_=in_), "dma")
        it.dsem = sem
        self.dcnt[sem] = self.dcnt.get(sem, 0) + 16
        it.dval = self.dcnt[sem]
        self._deps(it, r, w, extra)
        self.q[eng].append(it)
        return it

    def finalize(self):
        for e in self.ENG:
            for it in self.q[e]:
                for d in it.deps:
                    if d.kind == "op" and not (d.eng == "pe" and it.eng == "pe"):
                        d.marked = True
        for e in self.ENG:
            c = 0
            for it in self.q[e]:
                if it.kind == "op" and it.marked:
                    c += 1
                    it.count = c
        nw = 0
        for e in self.ENG:
            seen = {}
            for it in self.q[e]:
                ws = {}
                for d in it.deps:
                    if d.kind == "dma":
                        s, v = d.dsem, d.dval
                    else:
                        if d.eng == "pe" and it.eng == "pe":
                            continue
                        s, v = d.eng, d.count
                    if seen.get(s, 0) >= v:
                        continue
                    if ws.get(s, 0) < v:
                        ws[s] = v
                for s, v in ws.items():
                    seen[s] = v
                it.waits = list(ws.items())
                nw += len(it.waits)
        return nw


def build(S, n_tiles=None, dbg=False):
    NT = S // T if n_tiles is None else n_tiles
    NKT = S // 128
    nc = bass.Bass("TRN2", target_bir_lowering=False)

    def dt(name, shape, d=F32, kind="ExternalInput"):
        return nc.dram_tensor(name, shape, d, kind=kind).ap()

    x = dt("x", [S, D])
    out = dt("out", [S, D], kind="ExternalOutput")
    W = {}
    for f in (1, 2):
        W[("g", f)] = dt(f"ffn{f}_w_gate", [D, DFF])
        W[("u", f)] = dt(f"ffn{f}_w_up", [D, DFF])
        W[("d", f)] = dt(f"ffn{f}_w_down", [DFF, D])
    w_in = dt("w_in", [D, 2048])
    w_out = dt("w_out", [D, D])
    pool_w = dt("pool_w", [4, 128, 128])
    gains_d = dt("gains", [128, 24])
    gfin_d = dt("gfin", [128, D])
    lamv_d = dt("lamv", [128, 256])
    sgain_d = dt("sgain", [128, 128])
    pscale_d = dt("pscale", [128, 4])
    cosT = dt("cosT", [128, S])
    sinT = dt("sinT", [128, S])
    ident_d = dt("ident", [128, 128])
    maskm_d = dt("maskm", [128, 896], BF16)
    rcnt_d = dt("rcnt", [128, 16])
    dbg_t = {}
    if dbg:
        for nm in ("x1", "x2", "x3"):
            dbg_t[nm] = dt("dbg_" + nm, [S // T, 128, 4, D], kind="ExternalOutput")
        dbg_t["catT"] = dt("dbg_catT", [S // T, 128, 8, T], BF16, kind="ExternalOutput")
        dbg_t["KT"] = dt("dbg_KT", [128, 4, S], BF16, kind="ExternalOutput")
        dbg_t["dT"] = dt("dbg_dT", [S // T, 128, 4, T], BF16, kind="ExternalOutput")

    chunks = []
    CI = {}

    def add(key, **kw):
        CI[key] = len(chunks)
        chunks.append(kw)

    def add_ffn(f):
        for c in range(NCF):
            add(("g", f, c), kind="cols", w=W[("g", f)], c0=(c // 2) * 256, grp=("g", f, c // 2), sub=c % 2, gw=256)
            add(("u", f, c), kind="cols", w=W[("u", f)], c0=(c // 2) * 256, grp=("u", f, c // 2), sub=c % 2, gw=256)
        for c in range(NCF):
            add(("d", f, c), kind="rows", w=W[("d", f)], r0=c * 128, grp=("d", f, c), sub=0)

    add_ffn(1)
    for g in range(4):
        add(("U", g), kind="cols", w=w_in, c0=1536 + (g // 2) * 256, grp=("U", g // 2), sub=g % 2, gw=256)
    for v in range(4):
        add(("V", v), kind="v", v=v, grp=("V", v), sub=0)
    for h in range(4):
        add(("Q", h), kind="cols", w=w_in, c0=(h // 2) * 256, grp=("Q", h // 2), sub=h % 2, gw=256)
        add(("QS", h), kind="cols", w=w_in, c0=(h // 2) * 256, grp=("Q", h // 2), sub=h % 2, gw=256, swap=True)
        add(("K", h), kind="cols", w=w_in, c0=512 + (h // 2) * 256, grp=("K", h // 2), sub=h % 2, gw=256)
        add(("KS", h), kind="cols", w=w_in, c0=512 + (h // 2) * 256, grp=("K", h // 2), sub=h % 2, gw=256, swap=True)
    for kc in range(8):
        add(("WO", kc), kind="rows", w=w_out, r0=kc * 128, grp=("WO", kc), sub=0)
    add_ffn(2)
    NPER = len(chunks)
    add("PW", kind="pw", grp=("PW",), sub=0)
    NCH = len(chunks)
    wbf = dt("wbf", [NCH, 128, 1024], BF16, kind="Internal")

    P = Prog()
    es = contextlib.ExitStack()
    with es:
        def sb(name, shape, d):
            return es.enter_context(nc.sbuf_tensor(name, shape, d))

        KT = sb("KT", [128, 4, S], BF16)
        VC = sb("VC", [128, NKT, 4, 130], BF16)
        xres = sb("xres", [128, 4, D], F32)
        xnT = sb("xnT", [128, 8, T], BF16)
        ring = sb("ring", [128, NSLOT, 1024], BF16)
        arena = sb("arena", [128, ARENA_KIB * 256], F32)
        rstd_b = sb("rstd_b", [128, T], F32)
        ident = sb("ident_s", [128, 128], F32)
        identb = sb("identb", [128, 128], BF16)
        zerosb = sb("zerosb", [128, 128], BF16)
        ones32 = sb("ones32", [128, 128], F32)
        maskm = sb("maskm_s", [128, 896], BF16)
        sgain08 = sb("sgain08", [128, 128], F32)
        pwb = sb("pwb", [128, 512], BF16)
        gains = sb("gains_s", [128, 24], F32)
        pscale = sb("pscale_s", [128, 4], F32)
        rcnt = sb("rcnt_s", [128, 16], F32)
        halo = sb("halo", [128, 4, 16], F32)
        small = sb("small", [128, 64], F32)
        junkS = sb("junkS", [128, 128], BF16)
        tmpf = sb("tmpf", [128, 16], F32)
        psb = [es.enter_context(nc.psum_tensor(f"ps{b}", [128, 512], F32)) for b in range(8)]

        ssq = small[:, 0:4]
        mse = small[:, 4:8]
        rstd = small[:, 8:12]
        neghalf = small[:, 12:20]
        rr = small[:, 20:28]
        rn = small[:, 28:32]
        ssa = small[:, 32:36]
        msa = small[:, 36:40]
        rsa = small[:, 40:44]
        lsc = small[:, 44:52]
        neglam = small[:, 49:50]

        def av(off, nbytes, d):
            a = arena[:, off // 4:(off + nbytes) // 4]
            return a if d == F32 else a.bitcast(d)

        def blk(off, nbytes):
            return [("A", i) for i in range(off // 1024, (off + nbytes + 1023) // 1024)]

        KB = 1024
        gT = av(0, 22 * KB, BF16).rearrange("p (c n) -> p c n", n=T)
        sg = [av(22 * KB + i * 2 * KB, 2 * KB, F32) for i in range(2)]
        stage = av(0, 16 * KB, F32).rearrange("p (s d) -> p s d", d=D)
        R0 = av(27 * KB, 2 * KB, F32)
        R1 = av(29 * KB, 2 * KB, F32)
        gfinb = av(27 * KB, 4 * KB, F32)
        kR0, kR1 = blk(27 * KB, 2 * KB), blk(29 * KB, 2 * KB)
        QT = av(0, 8 * KB, BF16).rearrange("p (h m n) -> p h m n", h=4, m=2)
        poolT = av(8 * KB, 4 * KB, BF16).rearrange("p (g n) -> p g n", n=T)
        dT = av(12 * KB, 4 * KB, BF16).rearrange("p (g n) -> p g n", n=T)
        PT = av(16 * KB, NPT * KB, BF16).rearrange("p (b n) -> p b n", n=T)
        DmS = [av((16 + s_) * KB, 512, F32) for s_ in range(4)]
        UB = 2112
        uoff = [20 * KB, 20 * KB + UB, 20 * KB + 2 * UB]
        ubuf = [av(o, UB, F32) for o in uoff]
        kub = [blk(o, UB) for o in uoff]
        aB = R0.rearrange("p (q n) -> p q n", n=128)
        aA = R1[:, 0:256].rearrange("p (q n) -> p q n", n=128)
        an = R1[:, 256:512].bitcast(BF16).rearrange("p (q n) -> p q n", n=128)
        st32 = [av(b * 4 * KB, 4 * KB, F32) for b in range(4)]
        st16 = [av(16 * KB + b * 2 * KB, 2 * KB, BF16) for b in range(4)]
        junk_sb = av(20 * KB, 2 * KB, BF16)
        kjunk = blk(20 * KB, 2 * KB)
        ps3b = psb[3][:].bitcast(BF16)

        def kps(b):
            return ("ps", b)

        setup_items = []
        for (dst, src, key) in [
            (ident[:], ident_d, "ident"), (maskm[:], maskm_d, "maskm"), (gains[:], gains_d, "gains"),
            (pscale[:], pscale_d, "pscale"), (rcnt[:], rcnt_d, "rcnt"),
            (sgain08[:], sgain_d, "sgain08"), (R0[:, 0:256], lamv_d, "lamv"),
        ]:
            setup_items.append(P.dma("sp", "setup", out=dst, in_=src, w=[key] if key != "lamv" else kR0))
        fence = P.op("sp", lambda e: e.nop(), extra=setup_items)
        for key in ["ident", "maskm", "gains", "pscale", "rcnt", "sgain08"] + kR0:
            P.lastw[key] = setup_items[-1]
        P.op("dve", lambda e: e.memset(zerosb[:], 0.0), w=["zerosb"])
        P.op("dve", lambda e: e.memset(ones32[:], 1.0), w=["ones32"])
        P.op("dve", lambda e: e.memset(neghalf, -0.5), w=["neghalf"])
        P.op("dve", lambda e: e.memset(halo[:], 0.0), w=["halo"])
        P.op("pool", lambda e: e.memset(QT[:, :, :, :], 0.0), w=blk(0, 8 * KB))
        P.op("pool", lambda e: e.memset(VC[:, :, :, 128:129], 1.0), w=["VCinit"])
        P.op("pool", lambda e: e.memset(VC[:, :, :, 129:130], 0.0), w=["VCinit"])
        P.op("dve", lambda e: e.tensor_copy(out=identb[:], in_=ident[:]), r=["ident"], w=["identb"])
        P.op("dve", lambda e: e.tensor_scalar(out=sgain08[:], in0=sgain08[:], scalar1=0.8, scalar2=None,
                                              op0=ALU.mult), r=["sgain08"], w=["sgain08"])
        lam_in = R0[:, 0:256]
        lam_t = R0[:, 256:384]
        P.op("dve", lambda e: e.tensor_tensor(out=lam_t[:, 0:64], in0=lam_in[:, 0:64], in1=lam_in[:, 64:128],
                                              op=ALU.mult), r=kR0, w=kR0)
        P.op("dve", lambda e: e.tensor_tensor(out=lam_t[:, 64:128], in0=lam_in[:, 128:192], in1=lam_in[:, 192:256],
                                              op=ALU.mult), r=kR0, w=kR0)
        P.op("act", lambda e: e.activation(out=junkS[:, 0:64], in_=lam_t[:, 0:64], func=AF.Copy,
                                           accum_out=lsc[:, 0:1]), r=kR0, w=["junkS", "l1"])
        P.op("act", lambda e: e.activation(out=junkS[:, 64:128], in_=lam_t[:, 64:128], func=AF.Copy,
                                           accum_out=lsc[:, 1:2]), r=kR0, w=["junkS", "l2"])
        P.op("act", lambda e: e.activation(out=lsc[:, 2:4], in_=lsc[:, 0:2], func=AF.Exp), r=["l1", "l2"], w=["e12"])
        P.op("dve", lambda e: e.tensor_tensor(out=lsc[:, 4:5], in0=lsc[:, 3:4], in1=lsc[:, 2:3], op=ALU.subtract),
             r=["e12"], w=["lamd"])
        P.op("dve", lambda e: e.tensor_scalar(out=neglam, in0=lsc[:, 4:5], scalar1=-0.2, scalar2=None, op0=ALU.add),
             r=["lamd"], w=["neglam"])

        NB32, NB16 = 4, 12
        if S >= 8192:
            def ktreg(h, ti, ntl, d):
                a_ = KT[:, h, ti * 512:(ti + ntl) * 512]
                return (a_ if d == BF16 else a_.bitcast(d)), [("KT", h, tt) for tt in range(ti, ti + ntl)]
            st32r = [ktreg(h, 1, 8, F32) for h in range(4)]
            st16r = [ktreg(h, 9 + 2 * i, 2, BF16) for h in range(4) for i in range(3)]
        else:
            stg32 = sb("stg32", [128, NB32, 2048], F32)
            stg16 = sb("stg16", [128, NB16, 1024], BF16)
            st32r = [(stg32[:, b, :], [("stg32", b)]) for b in range(NB32)]
            st16r = [(stg16[:, b, :], [("stg16", b)]) for b in range(NB16)]

        grp_buf = {}
        grp_cnt = [0]

        def convert_chunk(ci, dst16, kdst16, ceng):
            ch = chunks[ci]
            kind = ch["kind"]
            gid = ch["grp"]
            if gid not in grp_buf:
                b32 = grp_cnt[0] % NB32
                grp_cnt[0] += 1
                s32, k32 = st32r[b32]
                if kind == "cols":
                    gw = ch["gw"]
                    src = ch["w"].rearrange("(kc p) n -> p kc n", p=128)[:, :, ch["c0"]:ch["c0"] + gw]
                    dst = s32[:, 0:8 * gw].rearrange("p (kc n) -> p kc n", n=gw)
                elif kind == "rows":
                    src = ch["w"][ch["r0"]:ch["r0"] + 128, :]
                    dst = s32[:, 0:1024]
                elif kind == "v":
                    v = ch["v"]
                    src = w_in.rearrange("(kc p) n -> p kc n", p=128)[:, 2 * v:2 * v + 2, 1024:1536]
                    dst = s32[:, 0:1024].rearrange("p (kc n) -> p kc n", n=512)
                else:
                    src = pool_w.rearrange("g p d -> p g d")
                    dst = s32[:, 0:512].rearrange("p (g d) -> p g d", d=128)
                P.dma("sp", f"pl{b32}", out=dst, in_=src, w=k32)
                grp_buf[gid] = (s32, k32)
            s32, k32 = grp_buf[gid]

            def cp(o, i):
                if ceng == "act":
                    P.op("act", lambda e: e.activation(out=o, in_=i, func=AF.Copy), r=k32, w=kdst16)
                else:
                    P.op(ceng, lambda e: e.tensor_copy(out=o, in_=i), r=k32, w=kdst16)

            if kind == "cols":
                gw = ch["gw"]
                sub = ch["sub"]
                iv = s32[:, 0:8 * gw].rearrange("p (kc n) -> p kc n", n=gw)[:, :, sub * 128:(sub + 1) * 128]
                ov = dst16.rearrange("p (kc n) -> p kc n", n=128)
                if ch.get("swap"):
                    i4 = iv.rearrange("p kc (m d) -> p kc m d", d=64)
                    o4 = ov.rearrange("p kc (m d) -> p kc m d", d=64)
                    cp(o4[:, :, :, 0:32], i4[:, :, :, 32:64])
                    cp(o4[:, :, :, 32:64], i4[:, :, :, 0:32])
                else:
                    cp(ov, iv)
            elif kind == "pw":
                cp(dst16[:, 0:512], s32[:, 0:512])
            else:
                cp(dst16[:, 0:1024], s32[:, 0:1024])

        conv_n = [0]
        convert_chunk(CI["PW"], pwb[:], ["pwb"], "pool")

        seq = [ci for _ in range(NT) for ci in range(NPER)]
        st = dict(issued=0, consumed=0, nxt=0)
        chunk_ap = {}
        pend_st = []
        STORE_LAG = 5

        def prefetch():
            while st["issued"] < len(seq):
                n = st["issued"]
                depth = NB16 if n < NPER else NSLOT
                if n >= st["consumed"] + depth:
                    break
                ci = seq[n]
                if n < NPER:
                    ap16, k16 = st16r[n % NB16]
                    convert_chunk(ci, ap16, k16, ("pool", "act", "dve")[n % 3])
                    pend_st.append((n, ci, ap16, k16))
                    while pend_st and (len(pend_st) > STORE_LAG or n == NPER - 1):
                        n2, ci2, a2, k2 = pend_st.pop(0)
                        P.dma("sp", f"pst{n2 % NB16}", out=wbf[ci2, :, :], in_=a2, r=k2, w=[("wbf", ci2)])
                    chunk_ap[n] = (ap16, k16)
                else:
                    slot = n % NSLOT
                    P.dma("sp", f"slot{slot}", out=ring[:, slot, :], in_=wbf[ci, :, :],
                          r=[("wbf", ci)], w=[("slot", slot)])
                    chunk_ap[n] = (ring[:, slot, :], [("slot", slot)])
                st["issued"] += 1

        def need(key):
            n = st["nxt"]
            assert seq[n] == CI[key], (key, n, seq[n], CI[key])
            st["nxt"] += 1
            if n >= st["issued"]:
                prefetch()
            assert n < st["issued"]
            return chunk_ap.pop(n)

        def done(k=1):
            st["consumed"] += k
            prefetch()

        def mm(o, lhsT, rhs, start, stop, r, w, **kw):
            return P.op("pe", lambda e: e.matmul(o, lhsT, rhs, start=start, stop=stop, **kw), r=r, w=w)

        def rms_stats(n_out=4):
            for s in range(4):
                P.op("act", lambda e, s=s: e.activation(out=junk_sb, in_=xres[:, s, :], func=AF.Square,
                                                        accum_out=ssq[:, s:s + 1]),
                     r=[("xres", s)], w=kjunk + [("ssq", s)])
            P.op("dve", lambda e: e.tensor_scalar(out=mse, in0=ssq, scalar1=1.0 / D, scalar2=EPS,
                                                  op0=ALU.mult, op1=ALU.add),
                 r=[("ssq", s) for s in range(4)], w=["mse"] + [("mse", s_) for s_ in range(4)])
            P.op("pool", lambda e: e.tensor_tensor(out=rstd, in0=mse, in1=neghalf[:, 0:4], op=ALU.pow),
                 r=["mse", "neghalf"], w=["rstd"] + [("rstd", s_) for s_ in range(4)])

        def norm_to_xnT(gi):
            for s in range(4):
                P.op("act", lambda e, s=s: e.activation(out=junk_sb, in_=xres[:, s, :], func=AF.Square,
                                                        accum_out=ssq[:, s:s + 1]),
                     r=[("xres", s)], w=kjunk + [("ssq", s)])
                P.op("dve", lambda e, s=s: e.tensor_scalar(out=mse[:, s:s + 1], in0=ssq[:, s:s + 1],
                                                           scalar1=1.0 / D, scalar2=EPS, op0=ALU.mult, op1=ALU.add),
                     r=[("ssq", s)], w=[("mse", s)])
                P.op("pool", lambda e, s=s: e.tensor_tensor(out=rstd[:, s:s + 1], in0=mse[:, s:s + 1],
                                                            in1=neghalf[:, 0:1], op=ALU.pow),
                     r=[("mse", s), "neghalf"], w=[("rstd", s)])
                P.op("dve", lambda e, s=s: e.tensor_scalar(out=DmS[s], in0=ident[:], scalar1=rstd[:, s:s + 1],
                                                           scalar2=None, op0=ALU.mult),
                     r=["ident", ("rstd", s)], w=[("A", 16 + s)])
                for kc in range(8):
                    mm(psb[kc][:, s * 128:(s + 1) * 128], xres[:, s, kc * 128:(kc + 1) * 128], DmS[s], True, True,
                       r=[("xres", s), ("A", 16 + s)], w=[kps(kc)])
            for kc in range(8):
                gcol = gains[:, gi * 8 + kc:gi * 8 + kc + 1]
                if kc % 2 == 0:
                    P.op("act", lambda e, kc=kc, gcol=gcol: e.activation(out=xnT[:, kc, :], in_=psb[kc][:],
                                                                         func=AF.Copy, scale=gcol),
                         r=[kps(kc), "gains"], w=[("xnT", kc)])
                else:
                    P.op("dve", lambda e, kc=kc, gcol=gcol: e.tensor_scalar(out=xnT[:, kc, :], in0=psb[kc][:],
                                                                            scalar1=gcol, scalar2=None, op0=ALU.mult),
                         r=[kps(kc), "gains"], w=[("xnT", kc)])

        def ffn(f, gi, pre_normed=False, out_stage=False):
            if not pre_normed:
                norm_to_xnT(gi)
            for c in range(NCF):
                par = c % 2
                for (nm, bank) in (("g", par), ("u", 2 + par)):
                    wap, wk = need((nm, f, c))
                    for kc in range(8):
                        mm(psb[bank][:], wap[:, kc * 128:(kc + 1) * 128], xnT[:, kc, :], kc == 0, kc == 7,
                           r=wk + [("xnT", kc)], w=[kps(bank)])
                    done()
                ksg = blk(22 * KB + par * 2 * KB, 2 * KB)
                P.op("act", lambda e, par=par: e.activation(out=sg[par], in_=psb[par][:], func=AF.Silu),
                     r=[kps(par)], w=ksg)
                P.op("dve", lambda e, par=par, c=c: e.tensor_tensor(out=gT[:, NCF - 1 - c, :], in0=psb[2 + par][:],
                                                                    in1=sg[par], op=ALU.mult),
                     r=[kps(2 + par)] + ksg, w=[("A", NCF - 1 - c)])
            for c in range(NCF):
                wap, wk = need(("d", f, c))
                for s in range(4):
                    for hf in range(2):
                        b = s * 2 + hf
                        mm(psb[b][:], gT[:, NCF - 1 - c, s * 128:(s + 1) * 128], wap[:, hf * 512:(hf + 1) * 512],
                           c == 0, c == NCF - 1, r=wk + [("A", NCF - 1 - c)], w=[kps(b)])
                done()
            for s in range(4):
                for hf in range(2):
                    b = s * 2 + hf
                    xs = xres[:, s, hf * 512:(hf + 1) * 512]
                    if out_stage:
                        xo = stage[:, s, hf * 512:(hf + 1) * 512]
                        P.op("dve", lambda e, b=b, xs=xs, xo=xo: e.scalar_tensor_tensor(
                            out=xo, in0=psb[b][:], scalar=0.5, in1=xs, op0=ALU.mult, op1=ALU.add),
                             r=[kps(b), ("xres", s)], w=blk(s * 4 * KB + hf * 2 * KB, 2 * KB))
                    else:
                        P.op("dve", lambda e, b=b, xs=xs: e.scalar_tensor_tensor(
                            out=xs, in0=psb[b][:], scalar=0.5, in1=xs, op0=ALU.mult, op1=ALU.add),
                             r=[kps(b), ("xres", s)], w=[("xres", s)])

        cnt = [0]

        def attention_tile(t):
            nk = 4 * (t + 1)
            per_head = 2 * nk
            units = [(h, j, m) for h in range(4) for j in range(nk) for m in range(2)]
            n = len(units)
            LA = 2
            info = {}

            def kQ(h, m):
                return ("A", 2 * h + m)

            def qk(u):
                h, j, m = units[u]
                diag = j >= 4 * t
                jj = j - 4 * t
                sbk = 4 + cnt[0] % 3
                pb = cnt[0] % NPT
                cnt[0] += 1
                info[u] = (sbk, pb)
                c0 = 128 * jj if diag else 0
                info[u] = (sbk, pb, c0)
                mm(psb[sbk][:, c0:512], KT[:, h, j * 128:(j + 1) * 128], QT[:, h, m, c0:512], True, not diag,
                   r=[("KT", h, j // 4), kQ(h, m)], w=[kps(sbk)])
                if diag:
                    mm(psb[sbk][:, c0:512], identb[:], maskm[:, 384:896 - 128 * jj], False, True,
                       r=["identb", "maskm"], w=[kps(sbk)])

            def ex(u):
                sbk, pb, c0 = info[u]
                P.op("act", lambda e, sbk=sbk, pb=pb, c0=c0: e.activation(out=PT[:, pb, c0:512],
                                                                          in_=psb[sbk][:, c0:512],
                                                                          func=AF.Exp, scale=0.125),
                     r=[kps(sbk)], w=[("A", 16 + pb)])

            def av(u):
                h, j, m = units[u]
                sbk, pb, _c0 = info[u]
                diag = j >= 4 * t
                jj = j - 4 * t
                for qs in range(jj if diag else 0, 4):
                    a = qs * 2 + m
                    bank, off = a // 3, (a % 3) * 130
                    last = (j == 4 * t + qs)
                    mm(psb[bank][:, off:off + 130], PT[:, pb, qs * 128:(qs + 1) * 128], VC[:, j, h, :],
                       False, last, r=[("A", 16 + pb), ("VC", j), "VCinit"], w=[kps(bank)], skip_group_check=True)

            def zero_O():
                for b in range(3):
                    mm(psb[b][:], zerosb[:], maskm[:, 0:512], True, False, r=["zerosb", "maskm"], w=[kps(b)],
                       skip_group_check=True)

            def epiA(h):
                for qs in range(4):
                    a1, a2 = qs * 2, qs * 2 + 1
                    b1, o1 = a1 // 3, (a1 % 3) * 130
                    b2, o2 = a2 // 3, (a2 % 3) * 130
                    P.op("dve", lambda e, qs=qs, b1=b1, o1=o1: e.reciprocal(out=rr[:, 2 * qs:2 * qs + 1],
                                                                            in_=psb[b1][:, o1 + 128:o1 + 129]),
                         r=[kps(b1)], w=[("rr", 2 * qs)])
                    P.op("dve", lambda e, qs=qs, b2=b2, o2=o2: e.reciprocal(out=rr[:, 2 * qs + 1:2 * qs + 2],
                                                                            in_=psb[b2][:, o2 + 128:o2 + 129]),
                         r=[kps(b2)], w=[("rr", 2 * qs + 1)])
                    P.op("dve", lambda e, qs=qs: e.tensor_scalar(out=rn[:, qs:qs + 1],
                                                                 in0=rr[:, 2 * qs + 1:2 * qs + 2],
                                                                 scalar1=neglam, scalar2=None, op0=ALU.mult),
                         r=[("rr", 2 * qs + 1), "neglam"], w=[("rn", qs)])
                    P.op("act", lambda e, qs=qs, b1=b1, o1=o1: e.activation(out=aA[:, qs % 2, :],
                                                                            in_=psb[b1][:, o1:o1 + 128],
                                                                            func=AF.Copy,
                                                                            scale=rr[:, 2 * qs:2 * qs + 1]),
                         r=[kps(b1), ("rr", 2 * qs)], w=[("aA", qs % 2)] + (kR1 if (h == 0 and qs == 0) else []))
                    P.op("dve", lambda e, qs=qs, b2=b2, o2=o2: e.scalar_tensor_tensor(
                        out=aB[:, qs, :], in0=psb[b2][:, o2:o2 + 128], scalar=rn[:, qs:qs + 1],
                        in1=aA[:, qs % 2, :], op0=ALU.mult, op1=ALU.add),
                         r=[kps(b2), ("rn", qs), ("aA", qs % 2)], w=[("aB", qs)] + (kR0 if (h == 0 and qs == 0) else []))
                    P.op("act", lambda e, qs=qs: e.activation(out=junkS[:], in_=aB[:, qs, :], func=AF.Square,
                                                              accum_out=ssa[:, qs:qs + 1]),
                         r=[("aB", qs)], w=["junkS", ("ssa", qs)])
                P.op("dve", lambda e: e.tensor_scalar(out=msa, in0=ssa, scalar1=1.0 / 128, scalar2=EPS,
                                                      op0=ALU.mult, op1=ALU.add),
                     r=[("ssa", q) for q in range(4)], w=["msa"])
                P.op("pool", lambda e: e.tensor_tensor(out=rsa, in0=msa, in1=neghalf[:, 0:4], op=ALU.pow),
                     r=["msa", "neghalf"], w=["rsa"])
                for qs in range(4):
                    P.op("dve", lambda e, qs=qs: e.scalar_tensor_tensor(
                        out=an[:, qs, :], in0=aB[:, qs, :], scalar=rsa[:, qs:qs + 1], in1=sgain08[:],
                        op0=ALU.mult, op1=ALU.mult),
                         r=[("aB", qs), "rsa", "sgain08"], w=[("an", qs)])

            def epiB(h):
                for qs in range(4):
                    P.op("pe", lambda e, qs=qs: e.transpose(out=ps3b[:, qs * 128:(qs + 1) * 128], in_=an[:, qs, :],
                                                            identity=identb[:]),
                         r=[("an", qs), "identb"], w=[kps(3)])
                P.op("act", lambda e: e.activation(out=QT[:, h, 0, :], in_=ps3b[:, 0:512], func=AF.Copy),
                     r=[kps(3)], w=[kQ(h, 0)])

            pending = [None]
            for u in range(min(LA, n)):
                qk(u)
            for u in range(n):
                h, j, m = units[u]
                pos = u % per_head
                if pos == 0:
                    zero_O()
                if u + LA < n:
                    qk(u + LA)
                ex(u)
                av(u)
                if pending[0] is not None and pos == min(7, per_head - 1):
                    epiB(pending[0])
                    pending[0] = None
                if pos == per_head - 1:
                    epiA(h)
                    pending[0] = h
            if pending[0] is not None:
                epiB(pending[0])

        def mixer(t):
            norm_to_xnT(1)
            cosb, sinb = R0, R1
            P.dma("sp", "cos", out=cosb, in_=cosT[:, t * T:(t + 1) * T], w=kR0)
            P.dma("sp", "sin", out=sinb, in_=sinT[:, t * T:(t + 1) * T], w=kR1)
            P.op("pool", lambda e: e.memset(QT[64:128, :, 0, :], 0.0), w=[("A", 2 * h) for h in range(4)])
            P.op("pool", lambda e: e.memset(QT[0:64, :, 1, :], 0.0), w=[("A", 2 * h + 1) for h in range(4)])
            uA, uB, uC = ubuf
            for g in range(4):
                wap, wk = need(("U", g))
                for kc in range(8):
                    mm(psb[4 + g][:], wap[:, kc * 128:(kc + 1) * 128], xnT[:, kc, :], kc == 0, kc == 7,
                       r=wk + [("xnT", kc)], w=[kps(4 + g)])
                done()
                P.op("act", lambda e, g=g: e.activation(out=uA[:, 16:528], in_=psb[4 + g][:], func=AF.Copy),
                     r=[kps(4 + g)], w=kub[0])
                P.op("pool", lambda e, g=g: e.tensor_copy(out=uA[:, 0:16], in_=halo[:, g, :]),
                     r=[("halo", g), "halo"], w=kub[0])
                P.op("pool", lambda e, g=g: e.tensor_copy(out=halo[:, g, :], in_=uA[:, 512:528]),
                     r=kub[0], w=[("halo", g)])
                cur, kcur = uA, kub[0]
                for lvl in range(g + 1):
                    sh = 1 << lvl
                    lo = 2 * sh - 1
                    dst, kdst = (uB, kub[1]) if lvl % 2 == 0 else (uC, kub[2])
                    P.op("pool", lambda e, cur=cur, dst=dst, lo=lo, sh=sh: e.tensor_tensor(
                        out=dst[:, lo:528], in0=cur[:, lo:528], in1=cur[:, lo - sh:528 - sh], op=ALU.add),
                         r=kcur, w=kdst)
                    cur, kcur = dst, kdst
                win = 2 << g
                P.op("dve", lambda e, g=g, cur=cur, win=win: e.scalar_tensor_tensor(
                    out=dT[:, g, :], in0=cur[:, 16:528], scalar=1.0 / win, in1=uA[:, 16:528],
                    op0=ALU.mult, op1=ALU.subtract), r=kcur + kub[0], w=[("A", 12 + g)])
                if t == 0:
                    n = win - 1
                    P.op("dve", lambda e, cur=cur, n=n: e.tensor_tensor(out=tmpf[:, 0:n], in0=cur[:, 16:16 + n],
                                                                        in1=rcnt[:, 0:n], op=ALU.mult),
                         r=kcur + ["rcnt"], w=["tmpf"])
                    P.op("dve", lambda e, g=g, n=n: e.tensor_tensor(out=dT[:, g, 0:n], in0=tmpf[:, 0:n],
                                                                    in1=uA[:, 16:16 + n], op=ALU.subtract),
                         r=["tmpf"] + kub[0], w=[("A", 12 + g)])
            vsl = [need(("V", v)) for v in range(4)]
            for v in range(4):
                for kk in range(2):
                    kc = 2 * v + kk
                    for s in range(4):
                        mm(psb[s][:], xnT[:, kc, s * 128:(s + 1) * 128], vsl[v][0][:, kk * 512:(kk + 1) * 512],
                           kc == 0, kc == 7, r=vsl[v][1] + [("xnT", kc)], w=[kps(s)])
                done()
            for s in range(4):
                P.op("act", lambda e, s=s: e.activation(out=VC[:, 4 * t + s, :, 0:128],
                                                        in_=psb[s][:].rearrange("p (h d) -> p h d", h=4),
                                                        func=AF.Copy),
                     r=[kps(s)], w=[("VC", 4 * t + s)])
            for h in range(4):
                bo = 4 if h % 2 == 0 else 0
                for (nm, bank) in (("Q", bo), ("QS", bo + 1), ("K", bo + 2), ("KS", bo + 3)):
                    wap, wk = need((nm, h))
                    for kc in range(8):
                        mm(psb[bank][:], wap[:, kc * 128:(kc + 1) * 128], xnT[:, kc, :], kc == 0, kc == 7,
                           r=wk + [("xnT", kc)], w=[kps(bank)])
                    done()
                for (b0, isq) in ((bo, True), (bo + 2, False)):
                    P.op("dve", lambda e, b0=b0: e.tensor_tensor(out=psb[b0][:], in0=psb[b0][:], in1=cosb,
                                                                 op=ALU.mult), r=[kps(b0)] + kR0, w=[kps(b0)])
                    P.op("dve", lambda e, b0=b0: e.tensor_tensor(out=rstd_b[:], in0=psb[b0 + 1][:], in1=sinb,
                                                                 op=ALU.mult), r=[kps(b0 + 1)] + kR1, w=["rstd_b"])
                    if isq:
                        P.op("dve", lambda e, h=h, b0=b0: e.tensor_tensor(out=QT[0:64, h, 0, :], in0=psb[b0][0:64, :],
                                                                          in1=rstd_b[0:64, :], op=ALU.add),
                             r=[kps(b0), "rstd_b"], w=[("A", 2 * h)])
                        P.op("dve", lambda e, h=h, b0=b0: e.tensor_tensor(out=QT[64:128, h, 1, :],
                                                                          in0=psb[b0][64:128, :],
                                                                          in1=rstd_b[64:128, :], op=ALU.add),
                             r=[kps(b0), "rstd_b"], w=[("A", 2 * h + 1)])
                    else:
                        P.op("dve", lambda e, h=h, b0=b0: e.tensor_tensor(out=KT[:, h, t * T:(t + 1) * T],
                                                                          in0=psb[b0][:], in1=rstd_b[:], op=ALU.add),
                             r=[kps(b0), "rstd_b"], w=[("KT", h, t)])
            attention_tile(t)
            for g in range(4):
                bank = 4 + g % 2
                mm(psb[bank][:], pwb[:, g * 128:(g + 1) * 128], dT[:, g, :], True, True,
                   r=["pwb", ("A", 12 + g)], w=[kps(bank)])
                P.op("act", lambda e, g=g, bank=bank: e.activation(out=poolT[:, g, :], in_=psb[bank][:], func=AF.Copy,
                                                                   scale=pscale[:, g:g + 1]),
                     r=[kps(bank), "pscale"], w=[("A", 8 + g)])
            if dbg:
                P.dma("sp", "dbg", out=dbg_t["catT"][t][:, 0:4, :], in_=QT[:, :, 0, :], r=[("A", i) for i in range(8)])
                P.dma("sp", "dbg", out=dbg_t["catT"][t][:, 4:8, :], in_=poolT[:, :, :], r=[("A", 8 + i) for i in range(4)])
                P.dma("sp", "dbg", out=dbg_t["dT"][t], in_=dT[:, :, :], r=[("A", 12 + i) for i in range(4)])
            for kc in range(8):
                wap, wk = need(("WO", kc))
                lhs = QT[:, kc, 0, :] if kc < 4 else poolT[:, kc - 4, :]
                kl = ("A", 2 * kc) if kc < 4 else ("A", 8 + kc - 4)
                for s in range(4):
                    for hf in range(2):
                        b = s * 2 + hf
                        mm(psb[b][:], lhs[:, s * 128:(s + 1) * 128], wap[:, hf * 512:(hf + 1) * 512],
                           kc == 0, kc == 7, r=wk + [kl], w=[kps(b)])
                done()
            for s in range(4):
                for hf in range(2):
                    b = s * 2 + hf
                    xs = xres[:, s, hf * 512:(hf + 1) * 512]
                    P.op("dve", lambda e, b=b, xs=xs: e.tensor_tensor(out=xs, in0=psb[b][:], in1=xs, op=ALU.add),
                         r=[kps(b), ("xres", s)], w=[("xres", s)])

        last_store = [None]

        ssqF = small[:, 52:56]
        mseF = small[:, 56:60]
        rstdF = small[:, 60:64]

        def final(t):
            P.dma("sp", "gfin", out=gfinb, in_=gfin_d,
                  w=kR0 + kR1 + [("aA", 0), ("aA", 1)] + [("aB", q) for q in range(4)] + [("an", q) for q in range(4)])
            for s in range(4):
                P.op("act", lambda e, s=s: e.activation(out=junk_sb, in_=stage[:, s, :], func=AF.Square,
                                                        accum_out=ssqF[:, s:s + 1]),
                     r=blk(s * 4 * KB, 4 * KB), w=kjunk + [("ssqF", s)])
            P.op("dve", lambda e: e.tensor_scalar(out=mseF, in0=ssqF, scalar1=1.0 / D, scalar2=EPS,
                                                  op0=ALU.mult, op1=ALU.add),
                 r=[("ssqF", s) for s in range(4)], w=["mseF"])
            P.op("pool", lambda e: e.tensor_tensor(out=rstdF, in0=mseF, in1=neghalf[:, 0:4], op=ALU.pow),
                 r=["mseF", "neghalf"], w=["rstdF"])
            for s in range(4):
                P.op("act", lambda e, s=s: e.activation(out=stage[:, s, :], in_=stage[:, s, :], func=AF.Copy,
                                                        scale=rstdF[:, s:s + 1]),
                     r=["rstdF"] + blk(s * 4 * KB, 4 * KB), w=blk(s * 4 * KB, 4 * KB))
            for s in range(4):
                P.op("pool", lambda e, s=s: e.tensor_tensor(out=stage[:, s, :], in0=stage[:, s, :], in1=gfinb,
                                                            op=ALU.mult),
                     r=kR0 + kR1 + blk(s * 4 * KB, 4 * KB), w=blk(s * 4 * KB, 4 * KB))
            last_store[0] = P.dma("act", "ost",
                                  out=out[t * T:(t + 1) * T, :].rearrange("(s p) d -> p s d", p=128),
                                  in_=stage, r=blk(0, 16 * KB))

        def load_x(t):
            for s in range(4):
                P.dma("sp", f"x{s}", out=xres[:, s, :], in_=x[t * T + s * 128:t * T + (s + 1) * 128, :],
                      w=[("xres", s)])

        load_x(0)
        norm_to_xnT(0)
        for t in range(NT):
            ffn(1, 0, pre_normed=True)
            mixer(t)
            ffn(2, 2, out_stage=True)
            if t + 1 < NT:
                load_x(t + 1)
                norm_to_xnT(0)
            final(t)
        P.op("sp", lambda e: e.nop(), extra=[last_store[0]])
        P.op("act", lambda e: e.nop(), extra=[last_store[0]])

        nwaits = P.finalize()
        sem_names = set(P.ENG) | set(P.dcnt.keys())
        sems = {n: es.enter_context(nc.semaphore(n)) for n in sorted(sem_names)}

        def replay(name):
            def run(e):
                for it in P.q[name]:
                    for (sn, v) in it.waits:
                        e.wait_ge(sems[sn], v)
                    ins = it.fn(e)
                    if it.kind == "dma":
                        ins.then_inc(sems[it.dsem], 16)
                    elif it.marked:
                        ins.then_inc(sems[it.eng], 1)
            return run

        with nc.Block() as block:
            block.tensor(replay("pe"))
            block.scalar(replay("act"))
            block.vector(replay("dve"))
            block.gpsimd(replay("pool"))
            block.sync(replay("sp"))
    nc._stats = {e: len(P.q[e]) for e in P.ENG}
    nc._stats["waits"] = nwaits
    return nc


def host_consts(S):
    inv_freq = (1.0 / (np.float32(10000.0) ** (np.arange(0, 64, 2, dtype=np.float32) / np.float32(64)))).astype(np.float32)
    pos = np.arange(S, dtype=np.float32)
    ang = (pos[:, None] * inv_freq[None, :]).astype(np.float32)
    c = np.cos(ang).astype(np.float32).T
    s = np.sin(ang).astype(np.float32).T
    cos64 = np.concatenate([c, c], axis=0)
    sin64 = np.concatenate([-s, s], axis=0)
    cosT = np.ascontiguousarray(np.concatenate([cos64, cos64], axis=0))
    sinT = np.ascontiguousarray(np.concatenate([sin64, sin64], axis=0))
    p = np.arange(128)[:, None]
    cc = np.arange(896)[None, :]
    maskm = np.where(cc >= 384 + 64 * (p >= 64), 0.0, -30000.0).astype(ml_dtypes.bfloat16)
    rcnt = np.broadcast_to((1.0 / np.arange(1, 17, dtype=np.float32))[None, :], (128, 16)).copy()
    return dict(cosT=cosT, sinT=sinT, ident=np.eye(128, dtype=np.float32), maskm=maskm, rcnt=rcnt)


def host_params(inp):
    f32 = lambda a: np.ascontiguousarray(np.asarray(a, dtype=np.float32))
    g3 = [f32(inp[k])[0].reshape(8, 128).T for k in ("ffn1_norm", "mix_norm", "ffn2_norm")]
    m = dict(
        gains=np.ascontiguousarray(np.concatenate(g3, axis=1)),
        gfin=np.ascontiguousarray(np.broadcast_to(f32(inp["final_norm"])[None, :], (128, D))),
        lamv=np.ascontiguousarray(np.broadcast_to(np.concatenate(
            [f32(inp[k])[0] for k in ("lambda_q1", "lambda_k1", "lambda_q2", "lambda_k2")])[None, :], (128, 256))),
        sgain=np.ascontiguousarray(np.broadcast_to(f32(inp["subln_gain"])[0][None, :], (128, 128))),
        pscale=np.ascontiguousarray(f32(inp["pool_scale"])[0].reshape(4, 128).T),
        w_in=f32(inp["w_in"])[0], w_out=f32(inp["w_out"])[0], pool_w=f32(inp["pool_w"])[0],
    )
    for f in (1, 2):
        for nm in ("gate", "up", "down"):
            m[f"ffn{f}_w_{nm}"] = f32(inp[f"ffn{f}_w_{nm}"])[0]
    return m


def kernel(**inputs):
    x = np.asarray(inputs["x"], dtype=np.float32)
    B, S, _ = x.shape
    nc = build(S)
    shared = host_params(inputs)
    shared.update(host_consts(S))
    in_maps = []
    for b in range(B):
        m = dict(shared)
        m["x"] = np.ascontiguousarray(x[b])
        in_maps.append(m)
    res = run_bass_kernel_spmd(nc, in_maps, core_ids=list(range(B)))
    return np.stack([np.asarray(r["out"], dtype=np.float32) for r in res.results], axis=0)
```

```python
import contextlib
import numpy as np
import ml_dtypes
import concourse.bass as bass
import concourse.mybir as mybir
from concourse.bass_utils import run_bass_kernel_spmd

F32 = mybir.dt.float32
BF16 = mybir.dt.bfloat16
AF = mybir.ActivationFunctionType
ALU = mybir.AluOpType

D = 1024
DFF = 2816
NCF = 22
T = 512
NSLOT = 8
NPT = 4
EPS = 1e-6
ARENA_KIB = 31


class Item:
    __slots__ = ("eng", "fn", "kind", "deps", "marked", "count", "dsem", "dval", "waits")

    def __init__(self, eng, fn, kind):
        self.eng = eng
        self.fn = fn
        self.kind = kind
        self.deps = []
        self.marked = False
        self.count = None
        self.dsem = None
        self.dval = None
        self.waits = []


class Prog:
    ENG = ["pe", "act", "dve", "pool", "sp"]

    def __init__(self):
        self.q = {e: [] for e in self.ENG}
        self.lastw = {}
        self.readers = {}
        self.dcnt = {}

    def _deps(self, it, r, w, extra):
        deps = {}
        for k in r:
            lw = self.lastw.get(k)
            if lw is not None:
                deps[id(lw)] = lw
        for k in w:
            lw = self.lastw.get(k)
            if lw is not None:
                deps[id(lw)] = lw
            for rd in self.readers.get(k, {}).values():
                deps[id(rd)] = rd
        for x in extra:
            if x is not None:
                deps[id(x)] = x
        deps.pop(id(it), None)
        it.deps = list(deps.values())
        rk = it.dsem if it.kind == "dma" else it.eng
        for k in r:
            self.readers.setdefault(k, {})[rk] = it
        for k in w:
            self.lastw[k] = it
            self.readers[k] = {}

    def op(self, eng, fn, r=(), w=(), extra=()):
        it = Item(eng, fn, "op")
        self._deps(it, r, w, extra)
        self.q[eng].append(it)
        return it

    def dma(self, eng, sem, out, in_, r=(), w=(), extra=()):
        it = Item(eng, lambda e: e.dma_start(out=out, in_=in_), "dma")
        it.dsem = sem
        self.dcnt[sem] = self.dcnt.get(sem, 0) + 16
        it.dval = self.dcnt[sem]
        self._deps(it, r, w, extra)
        self.q[eng].append(it)
        return it

    def finalize(self):
        for e in self.ENG:
            for it in self.q[e]:
                for d in it.deps:
                    if d.kind == "op" and not (d.eng == "pe" and it.eng == "pe"):
                        d.marked = True
        for e in self.ENG:
            c = 0
            for it in self.q[e]:
                if it.kind == "op" and it.marked:
                    c += 1
                    it.count = c
        nw = 0
        for e in self.ENG:
            seen = {}
            for it in self.q[e]:
                ws = {}
                for d in it.deps:
                    if d.kind == "dma":
                        s, v = d.dsem, d.dval
                    else:
                        if d.eng == "pe" and it.eng == "pe":
                            continue
                        s, v = d.eng, d.count
                    if seen.get(s, 0) >= v:
                        continue
                    if ws.get(s, 0) < v:
                        ws[s] = v
                for s, v in ws.items():
                    seen[s] = v
                it.waits = list(ws.items())
                nw += len(it.waits)
        return nw


def build(S, n_tiles=None, dbg=False):
    NT = S // T if n_tiles is None else n_tiles
    NKT = S // 128
    nc = bass.Bass("TRN2", target_bir_lowering=False)

    def dt(name, shape, d=F32, kind="ExternalInput"):
        return nc.dram_tensor(name, shape, d, kind=kind).ap()

    x = dt("x", [S, D])
    out = dt("out", [S, D], kind="ExternalOutput")
    W = {}
    for f in (1, 2):
        W[("g", f)] = dt(f"ffn{f}_w_gate", [D, DFF])
        W[("u", f)] = dt(f"ffn{f}_w_up", [D, DFF])
        W[("d", f)] = dt(f"ffn{f}_w_down", [DFF, D])
    w_in = dt("w_in", [D, 2048])
    w_out = dt("w_out", [D, D])
    pool_w = dt("pool_w", [4, 128, 128])
    gains_d = dt("gains", [128, 24])
    gfin_d = dt("gfin", [128, D])
    lamv_d = dt("lamv", [128, 256])
    sgain_d = dt("sgain", [128, 128])
    pscale_d = dt("pscale", [128, 4])
    cosT = dt("cosT", [128, S])
    sinT = dt("sinT", [128, S])
    ident_d = dt("ident", [128, 128])
    maskm_d = dt("maskm", [128, 896], BF16)
    rcnt_d = dt("rcnt", [128, 16])
    dbg_t = {}
    if dbg:
        for nm in ("x1", "x2", "x3"):
            dbg_t[nm] = dt("dbg_" + nm, [S // T, 128, 4, D], kind="ExternalOutput")
        dbg_t["catT"] = dt("dbg_catT", [S // T, 128, 8, T], BF16, kind="ExternalOutput")
        dbg_t["KT"] = dt("dbg_KT", [128, 4, S], BF16, kind="ExternalOutput")
        dbg_t["dT"] = dt("dbg_dT", [S // T, 128, 4, T], BF16, kind="ExternalOutput")

    chunks = []
    CI = {}

    def add(key, **kw):
        CI[key] = len(chunks)
        chunks.append(kw)

    def add_ffn(f):
        for c in range(NCF):
            add(("g", f, c), kind="cols", w=W[("g", f)], c0=(c // 2) * 256, grp=("g", f, c // 2), sub=c % 2, gw=256)
            add(("u", f, c), kind="cols", w=W[("u", f)], c0=(c // 2) * 256, grp=("u", f, c // 2), sub=c % 2, gw=256)
        for c in range(NCF):
            add(("d", f, c), kind="rows", w=W[("d", f)], r0=c * 128, grp=("d", f, c), sub=0)

    add_ffn(1)
    for g in range(4):
        add(("U", g), kind="cols", w=w_in, c0=1536 + (g // 2) * 256, grp=("U", g // 2), sub=g % 2, gw=256)
    for v in range(4):
        add(("V", v), kind="v", v=v, grp=("V", v), sub=0)
    for h in range(4):
        add(("Q", h), kind="cols", w=w_in, c0=(h // 2) * 256, grp=("Q", h // 2), sub=h % 2, gw=256)
        add(("QS", h), kind="cols", w=w_in, c0=(h // 2) * 256, grp=("Q", h // 2), sub=h % 2, gw=256, swap=True)
        add(("K", h), kind="cols", w=w_in, c0=512 + (h // 2) * 256, grp=("K", h // 2), sub=h % 2, gw=256)
        add(("KS", h), kind="cols", w=w_in, c0=512 + (h // 2) * 256, grp=("K", h // 2), sub=h % 2, gw=256, swap=True)
    for kc in range(8):
        add(("WO", kc), kind="rows", w=w_out, r0=kc * 128, grp=("WO", kc), sub=0)
    add_ffn(2)
    NPER = len(chunks)
    add("PW", kind="pw", grp=("PW",), sub=0)
    NCH = len(chunks)
    wbf = dt("wbf", [NCH, 128, 1024], BF16, kind="Internal")

    P = Prog()
    es = contextlib.ExitStack()
    with es:
        def sb(name, shape, d):
            return es.enter_context(nc.sbuf_tensor(name, shape, d))

        KT = sb("KT", [128, 4, S], BF16)
        VC = sb("VC", [128, NKT, 4, 130], BF16)
        xres = sb("xres", [128, 4, D], F32)
        xnT = sb("xnT", [128, 8, T], BF16)
        ring = sb("ring", [128, NSLOT, 1024], BF16)
        arena = sb("arena", [128, ARENA_KIB * 256], F32)
        rstd_b = sb("rstd_b", [128, T], F32)
        ident = sb("ident_s", [128, 128], F32)
        identb = sb("identb", [128, 128], BF16)
        zerosb = sb("zerosb", [128, 128], BF16)
        ones32 = sb("ones32", [128, 128], F32)
        maskm = sb("maskm_s", [128, 896], BF16)
        sgain08 = sb("sgain08", [128, 128], F32)
        pwb = sb("pwb", [128, 512], BF16)
        gains = sb("gains_s", [128, 24], F32)
        pscale = sb("pscale_s", [128, 4], F32)
        rcnt = sb("rcnt_s", [128, 16], F32)
        halo = sb("halo", [128, 4, 16], F32)
        small = sb("small", [128, 64], F32)
        junkS = sb("junkS", [128, 128], BF16)
        tmpf = sb("tmpf", [128, 16], F32)
        psb = [es.enter_context(nc.psum_tensor(f"ps{b}", [128, 512], F32)) for b in range(8)]

        ssq = small[:, 0:4]
        mse = small[:, 4:8]
        rstd = small[:, 8:12]
        neghalf = small[:, 12:20]
        rr = small[:, 20:28]
        rn = small[:, 28:32]
        ssa = small[:, 32:36]
        msa = small[:, 36:40]
        rsa = small[:, 40:44]
        lsc = small[:, 44:52]
        neglam = small[:, 49:50]

        def av(off, nbytes, d):
            a = arena[:, off // 4:(off + nbytes) // 4]
            return a if d == F32 else a.bitcast(d)

        def blk(off, nbytes):
            return [("A", i) for i in range(off // 1024, (off + nbytes + 1023) // 1024)]

        KB = 1024
        gT = av(0, 22 * KB, BF16).rearrange("p (c n) -> p c n", n=T)
        sg = [av(22 * KB + i * 2 * KB, 2 * KB, F32) for i in range(2)]
        stage = av(0, 16 * KB, F32).rearrange("p (s d) -> p s d", d=D)
        R0 = av(27 * KB, 2 * KB, F32)
        R1 = av(29 * KB, 2 * KB, F32)
        gfinb = av(27 * KB, 4 * KB, F32)
        kR0, kR1 = blk(27 * KB, 2 * KB), blk(29 * KB, 2 * KB)
        QT = av(0, 8 * KB, BF16).rearrange("p (h m n) -> p h m n", h=4, m=2)
        poolT = av(8 * KB, 4 * KB, BF16).rearrange("p (g n) -> p g n", n=T)
        dT = av(12 * KB, 4 * KB, BF16).rearrange("p (g n) -> p g n", n=T)
        PT = av(16 * KB, NPT * KB, BF16).rearrange("p (b n) -> p b n", n=T)
        DmS = [av((16 + s_) * KB, 512, F32) for s_ in range(4)]
        UB = 2112
        uoff = [20 * KB, 20 * KB + UB, 20 * KB + 2 * UB]
        ubuf = [av(o, UB, F32) for o in uoff]
        kub = [blk(o, UB) for o in uoff]
        aB = R0.rearrange("p (q n) -> p q n", n=128)
        aA = R1[:, 0:256].rearrange("p (q n) -> p q n", n=128)
        an = R1[:, 256:512].bitcast(BF16).rearrange("p (q n) -> p q n", n=128)
        st32 = [av(b * 4 * KB, 4 * KB, F32) for b in range(4)]
        st16 = [av(16 * KB + b * 2 * KB, 2 * KB, BF16) for b in range(4)]
        junk_sb = av(20 * KB, 2 * KB, BF16)
        kjunk = blk(20 * KB, 2 * KB)
        ps3b = psb[3][:].bitcast(BF16)

        def kps(b):
            return ("ps", b)

        setup_items = []
        for (dst, src, key) in [
            (ident[:], ident_d, "ident"), (maskm[:], maskm_d, "maskm"), (gains[:], gains_d, "gains"),
            (pscale[:], pscale_d, "pscale"), (rcnt[:], rcnt_d, "rcnt"),
            (sgain08[:], sgain_d, "sgain08"), (R0[:, 0:256], lamv_d, "lamv"),
        ]:
            setup_items.append(P.dma("sp", "setup", out=dst, in_=src, w=[key] if key != "lamv" else kR0))
        fence = P.op("sp", lambda e: e.nop(), extra=setup_items)
        for key in ["ident", "maskm", "gains", "pscale", "rcnt", "sgain08"] + kR0:
            P.lastw[key] = setup_items[-1]
        P.op("dve", lambda e: e.memset(zerosb[:], 0.0), w=["zerosb"])
        P.op("dve", lambda e: e.memset(ones32[:], 1.0), w=["ones32"])
        P.op("dve", lambda e: e.memset(neghalf, -0.5), w=["neghalf"])
        P.op("dve", lambda e: e.memset(halo[:], 0.0), w=["halo"])
        P.op("pool", lambda e: e.memset(QT[:, :, :, :], 0.0), w=blk(0, 8 * KB))
        P.op("pool", lambda e: e.memset(VC[:, :, :, 128:129], 1.0), w=["VCinit"])
        P.op("pool", lambda e: e.memset(VC[:, :, :, 129:130], 0.0), w=["VCinit"])
        P.op("dve", lambda e: e.tensor_copy(out=identb[:], in_=ident[:]), r=["ident"], w=["identb"])
        P.op("dve", lambda e: e.tensor_scalar(out=sgain08[:], in0=sgain08[:], scalar1=0.8, scalar2=None,
                                              op0=ALU.mult), r=["sgain08"], w=["sgain08"])
        lam_in = R0[:, 0:256]
        lam_t = R0[:, 256:384]
        P.op("dve", lambda e: e.tensor_tensor(out=lam_t[:, 0:64], in0=lam_in[:, 0:64], in1=lam_in[:, 64:128],
                                              op=ALU.mult), r=kR0, w=kR0)
        P.op("dve", lambda e: e.tensor_tensor(out=lam_t[:, 64:128], in0=lam_in[:, 128:192], in1=lam_in[:, 192:256],
                                              op=ALU.mult), r=kR0, w=kR0)
        P.op("act", lambda e: e.activation(out=junkS[:, 0:64], in_=lam_t[:, 0:64], func=AF.Copy,
                                           accum_out=lsc[:, 0:1]), r=kR0, w=["junkS", "l1"])
        P.op("act", lambda e: e.activation(out=junkS[:, 64:128], in_=lam_t[:, 64:128], func=AF.Copy,
                                           accum_out=lsc[:, 1:2]), r=kR0, w=["junkS", "l2"])
        P.op("act", lambda e: e.activation(out=lsc[:, 2:4], in_=lsc[:, 0:2], func=AF.Exp), r=["l1", "l2"], w=["e12"])
        P.op("dve", lambda e: e.tensor_tensor(out=lsc[:, 4:5], in0=lsc[:, 3:4], in1=lsc[:, 2:3], op=ALU.subtract),
             r=["e12"], w=["lamd"])
        P.op("dve", lambda e: e.tensor_scalar(out=neglam, in0=lsc[:, 4:5], scalar1=-0.2, scalar2=None, op0=ALU.add),
             r=["lamd"], w=["neglam"])

        NB32, NB16 = 4, 12
        if S >= 8192:
            def ktreg(h, ti, ntl, d):
                a_ = KT[:, h, ti * 512:(ti + ntl) * 512]
                return (a_ if d == BF16 else a_.bitcast(d)), [("KT", h, tt) for tt in range(ti, ti + ntl)]
            st32r = [ktreg(h, 1, 8, F32) for h in range(4)]
            st16r = [ktreg(h, 9 + 2 * i, 2, BF16) for h in range(4) for i in range(3)]
        else:
            stg32 = sb("stg32", [128, NB32, 2048], F32)
            stg16 = sb("stg16", [128, NB16, 1024], BF16)
            st32r = [(stg32[:, b, :], [("stg32", b)]) for b in range(NB32)]
            st16r = [(stg16[:, b, :], [("stg16", b)]) for b in range(NB16)]

        grp_buf = {}
        grp_cnt = [0]

        def convert_chunk(ci, dst16, kdst16, ceng):
            ch = chunks[ci]
            kind = ch["kind"]
            gid = ch["grp"]
            if gid not in grp_buf:
                b32 = grp_cnt[0] % NB32
                grp_cnt[0] += 1
                s32, k32 = st32r[b32]
                if kind == "cols":
                    gw = ch["gw"]
                    src = ch["w"].rearrange("(kc p) n -> p kc n", p=128)[:, :, ch["c0"]:ch["c0"] + gw]
                    dst = s32[:, 0:8 * gw].rearrange("p (kc n) -> p kc n", n=gw)
                elif kind == "rows":
                    src = ch["w"][ch["r0"]:ch["r0"] + 128, :]
                    dst = s32[:, 0:1024]
                elif kind == "v":
                    v = ch["v"]
                    src = w_in.rearrange("(kc p) n -> p kc n", p=128)[:, 2 * v:2 * v + 2, 1024:1536]
                    dst = s32[:, 0:1024].rearrange("p (kc n) -> p kc n", n=512)
                else:
                    src = pool_w.rearrange("g p d -> p g d")
                    dst = s32[:, 0:512].rearrange("p (g d) -> p g d", d=128)
                P.dma("sp", f"pl{b32}", out=dst, in_=src, w=k32)
                grp_buf[gid] = (s32, k32)
            s32, k32 = grp_buf[gid]

            def cp(o, i):
                if ceng == "act":
                    P.op("act", lambda e: e.activation(out=o, in_=i, func=AF.Copy), r=k32, w=kdst16)
                else:
                    P.op(ceng, lambda e: e.tensor_copy(out=o, in_=i), r=k32, w=kdst16)

            if kind == "cols":
                gw = ch["gw"]
                sub = ch["sub"]
                iv = s32[:, 0:8 * gw].rearrange("p (kc n) -> p kc n", n=gw)[:, :, sub * 128:(sub + 1) * 128]
                ov = dst16.rearrange("p (kc n) -> p kc n", n=128)
                if ch.get("swap"):
                    i4 = iv.rearrange("p kc (m d) -> p kc m d", d=64)
                    o4 = ov.rearrange("p kc (m d) -> p kc m d", d=64)
                    cp(o4[:, :, :, 0:32], i4[:, :, :, 32:64])
                    cp(o4[:, :, :, 32:64], i4[:, :, :, 0:32])
                else:
                    cp(ov, iv)
            elif kind == "pw":
                cp(dst16[:, 0:512], s32[:, 0:512])
            else:
                cp(dst16[:, 0:1024], s32[:, 0:1024])

        conv_n = [0]
        convert_chunk(CI["PW"], pwb[:], ["pwb"], "pool")

        seq = [ci for _ in range(NT) for ci in range(NPER)]
        st = dict(issued=0, consumed=0, nxt=0)
        chunk_ap = {}
        pend_st = []
        STORE_LAG = 5

        def prefetch():
            while st["issued"] < len(seq):
                n = st["issued"]
                depth = NB16 if n < NPER else NSLOT
                if n >= st["consumed"] + depth:
                    break
                ci = seq[n]
                if n < NPER:
                    ap16, k16 = st16r[n % NB16]
                    convert_chunk(ci, ap16, k16, ("pool", "act", "dve")[n % 3])
                    pend_st.append((n, ci, ap16, k16))
                    while pend_st and (len(pend_st) > STORE_LAG or n == NPER - 1):
                        n2, ci2, a2, k2 = pend_st.pop(0)
                        P.dma("sp", f"pst{n2 % NB16}", out=wbf[ci2, :, :], in_=a2, r=k2, w=[("wbf", ci2)])
                    chunk_ap[n] = (ap16, k16)
                else:
                    slot = n % NSLOT
                    P.dma("sp", f"slot{slot}", out=ring[:, slot, :], in_=wbf[ci, :, :],
                          r=[("wbf", ci)], w=[("slot", slot)])
                    chunk_ap[n] = (ring[:, slot, :], [("slot", slot)])
                st["issued"] += 1

        def need(key):
            n = st["nxt"]
            assert seq[n] == CI[key], (key, n, seq[n], CI[key])
            st["nxt"] += 1
            if n >= st["issued"]:
                prefetch()
            assert n < st["issued"]
            return chunk_ap.pop(n)

        def done(k=1):
            st["consumed"] += k
            prefetch()

        def mm(o, lhsT, rhs, start, stop, r, w, **kw):
            return P.op("pe", lambda e: e.matmul(o, lhsT, rhs, start=start, stop=stop, **kw), r=r, w=w)

        def rms_stats(n_out=4):
            for s in range(4):
                P.op("act", lambda e, s=s: e.activation(out=junk_sb, in_=xres[:, s, :], func=AF.Square,
                                                        accum_out=ssq[:, s:s + 1]),
                     r=[("xres", s)], w=kjunk + [("ssq", s)])
            P.op("dve", lambda e: e.tensor_scalar(out=mse, in0=ssq, scalar1=1.0 / D, scalar2=EPS,
                                                  op0=ALU.mult, op1=ALU.add),
                 r=[("ssq", s) for s in range(4)], w=["mse"] + [("mse", s_) for s_ in range(4)])
            P.op("pool", lambda e: e.tensor_tensor(out=rstd, in0=mse, in1=neghalf[:, 0:4], op=ALU.pow),
                 r=["mse", "neghalf"], w=["rstd"] + [("rstd", s_) for s_ in range(4)])

        def norm_to_xnT(gi):
            for s in range(4):
                P.op("act", lambda e, s=s: e.activation(out=junk_sb, in_=xres[:, s, :], func=AF.Square,
                                                        accum_out=ssq[:, s:s + 1]),
                     r=[("xres", s)], w=kjunk + [("ssq", s)])
                P.op("dve", lambda e, s=s: e.tensor_scalar(out=mse[:, s:s + 1], in0=ssq[:, s:s + 1],
                                                           scalar1=1.0 / D, scalar2=EPS, op0=ALU.mult, op1=ALU.add),
                     r=[("ssq", s)], w=[("mse", s)])
                P.op("pool", lambda e, s=s: e.tensor_tensor(out=rstd[:, s:s + 1], in0=mse[:, s:s + 1],
                                                            in1=neghalf[:, 0:1], op=ALU.pow),
                     r=[("mse", s), "neghalf"], w=[("rstd", s)])
                P.op("dve", lambda e, s=s: e.tensor_scalar(out=DmS[s], in0=ident[:], scalar1=rstd[:, s:s + 1],
                                                           scalar2=None, op0=ALU.mult),
                     r=["ident", ("rstd", s)], w=[("A", 16 + s)])
                for kc in range(8):
                    mm(psb[kc][:, s * 128:(s + 1) * 128], xres[:, s, kc * 128:(kc + 1) * 128], DmS[s], True, True,
                       r=[("xres", s), ("A", 16 + s)], w=[kps(kc)])
            for kc in range(8):
                gcol = gains[:, gi * 8 + kc:gi * 8 + kc + 1]
                if kc % 2 == 0:
                    P.op("act", lambda e, kc=kc, gcol=gcol: e.activation(out=xnT[:, kc, :], in_=psb[kc][:],
                                                                         func=AF.Copy, scale=gcol),
                         r=[kps(kc), "gains"], w=[("xnT", kc)])
                else:
                    P.op("dve", lambda e, kc=kc, gcol=gcol: e.tensor_scalar(out=xnT[:, kc, :], in0=psb[kc][:],
                                                                            scalar1=gcol, scalar2=None, op0=ALU.mult),
                         r=[kps(kc), "gains"], w=[("xnT", kc)])

        def ffn(f, gi, pre_normed=False, out_stage=False):
            if not pre_normed:
                norm_to_xnT(gi)
            for c in range(NCF):
                par = c % 2
                for (nm, bank) in (("g", par), ("u", 2 + par)):
                    wap, wk = need((nm, f, c))
                    for kc in range(8):
                        mm(psb[bank][:], wap[:, kc * 128:(kc + 1) * 128], xnT[:, kc, :], kc == 0, kc == 7,
                           r=wk + [("xnT", kc)], w=[kps(bank)])
                    done()
                ksg = blk(22 * KB + par * 2 * KB, 2 * KB)
                P.op("act", lambda e, par=par: e.activation(out=sg[par], in_=psb[par][:], func=AF.Silu),
                     r=[kps(par)], w=ksg)
                P.op("dve", lambda e, par=par, c=c: e.tensor_tensor(out=gT[:, NCF - 1 - c, :], in0=psb[2 + par][:],
                                                                    in1=sg[par], op=ALU.mult),
                     r=[kps(2 + par)] + ksg, w=[("A", NCF - 1 - c)])
            for c in range(NCF):
                wap, wk = need(("d", f, c))
                for s in range(4):
                    for hf in range(2):
                        b = s * 2 + hf
                        mm(psb[b][:], gT[:, NCF - 1 - c, s * 128:(s + 1) * 128], wap[:, hf * 512:(hf + 1) * 512],
                           c == 0, c == NCF - 1, r=wk + [("A", NCF - 1 - c)], w=[kps(b)])
                done()
            for s in range(4):
                for hf in range(2):
                    b = s * 2 + hf
                    xs = xres[:, s, hf * 512:(hf + 1) * 512]
                    if out_stage:
                        xo = stage[:, s, hf * 512:(hf + 1) * 512]
                        P.op("dve", lambda e, b=b, xs=xs, xo=xo: e.scalar_tensor_tensor(
                            out=xo, in0=psb[b][:], scalar=0.5, in1=xs, op0=ALU.mult, op1=ALU.add),
                             r=[kps(b), ("xres", s)], w=blk(s * 4 * KB + hf * 2 * KB, 2 * KB))
                    else:
                        P.op("dve", lambda e, b=b, xs=xs: e.scalar_tensor_tensor(
                            out=xs, in0=psb[b][:], scalar=0.5, in1=xs, op0=ALU.mult, op1=ALU.add),
                             r=[kps(b), ("xres", s)], w=[("xres", s)])

        cnt = [0]

        def attention_tile(t):
            nk = 4 * (t + 1)
            per_head = 2 * nk
            units = [(h, j, m) for h in range(4) for j in range(nk) for m in range(2)]
            n = len(units)
            LA = 2
            info = {}

            def kQ(h, m):
                return ("A", 2 * h + m)

            def qk(u):
                h, j, m = units[u]
                diag = j >= 4 * t
                jj = j - 4 * t
                sbk = 4 + cnt[0] % 3
                pb = cnt[0] % NPT
                cnt[0] += 1
                info[u] = (sbk, pb)
                c0 = 128 * jj if diag else 0
                info[u] = (sbk, pb, c0)
                mm(psb[sbk][:, c0:512], KT[:, h, j * 128:(j + 1) * 128], QT[:, h, m, c0:512], True, not diag,
                   r=[("KT", h, j // 4), kQ(h, m)], w=[kps(sbk)])
                if diag:
                    mm(psb[sbk][:, c0:512], identb[:], maskm[:, 384:896 - 128 * jj], False, True,
                       r=["identb", "maskm"], w=[kps(sbk)])

            def ex(u):
                sbk, pb, c0 = info[u]
                P.op("act", lambda e, sbk=sbk, pb=pb, c0=c0: e.activation(out=PT[:, pb, c0:512],
                                                                          in_=psb[sbk][:, c0:512],
                                                                          func=AF.Exp, scale=0.125),
                     r=[kps(sbk)], w=[("A", 16 + pb)])

            def av(u):
                h, j, m = units[u]
                sbk, pb, _c0 = info[u]
                diag = j >= 4 * t
                jj = j - 4 * t
                for qs in range(jj if diag else 0, 4):
                    a = qs * 2 + m
                    bank, off = a // 3, (a % 3) * 130
                    last = (j == 4 * t + qs)
                    mm(psb[bank][:, off:off + 130], PT[:, pb, qs * 128:(qs + 1) * 128], VC[:, j, h, :],
                       False, last, r=[("A", 16 + pb), ("VC", j), "VCinit"], w=[kps(bank)], skip_group_check=True)

            def zero_O():
                for b in range(3):
                    mm(psb[b][:], zerosb[:], maskm[:, 0:512], True, False, r=["zerosb", "maskm"], w=[kps(b)],
                       skip_group_check=True)

            def epiA(h):
                for qs in range(4):
                    a1, a2 = qs * 2, qs * 2 + 1
                    b1, o1 = a1 // 3, (a1 % 3) * 130
                    b2, o2 = a2 // 3, (a2 % 3) * 130
                    P.op("dve", lambda e, qs=qs, b1=b1, o1=o1: e.reciprocal(out=rr[:, 2 * qs:2 * qs + 1],
                                                                            in_=psb[b1][:, o1 + 128:o1 + 129]),
                         r=[kps(b1)], w=[("rr", 2 * qs)])
                    P.op("dve", lambda e, qs=qs, b2=b2, o2=o2: e.reciprocal(out=rr[:, 2 * qs + 1:2 * qs + 2],
                                                                            in_=psb[b2][:, o2 + 128:o2 + 129]),
                         r=[kps(b2)], w=[("rr", 2 * qs + 1)])
                    P.op("dve", lambda e, qs=qs: e.tensor_scalar(out=rn[:, qs:qs + 1],
                                                                 in0=rr[:, 2 * qs + 1:2 * qs + 2],
                                                                 scalar1=neglam, scalar2=None, op0=ALU.mult),
                         r=[("rr", 2 * qs + 1), "neglam"], w=[("rn", qs)])
                    P.op("act", lambda e, qs=qs, b1=b1, o1=o1: e.activation(out=aA[:, qs % 2, :],
                                                                            in_=psb[b1][:, o1:o1 + 128],
                                                                            func=AF.Copy,
                                                                            scale=rr[:, 2 * qs:2 * qs + 1]),
                         r=[kps(b1), ("rr", 2 * qs)], w=[("aA", qs % 2)] + (kR1 if (h == 0 and qs == 0) else []))
                    P.op("dve", lambda e, qs=qs, b2=b2, o2=o2: e.scalar_tensor_tensor(
                        out=aB[:, qs, :], in0=psb[b2][:, o2:o2 + 128], scalar=rn[:, qs:qs + 1],
                        in1=aA[:, qs % 2, :], op0=ALU.mult, op1=ALU.add),
                         r=[kps(b2), ("rn", qs), ("aA", qs % 2)], w=[("aB", qs)] + (kR0 if (h == 0 and qs == 0) else []))
                    P.op("act", lambda e, qs=qs: e.activation(out=junkS[:], in_=aB[:, qs, :], func=AF.Square,
                                                              accum_out=ssa[:, qs:qs + 1]),
                         r=[("aB", qs)], w=["junkS", ("ssa", qs)])
                P.op("dve", lambda e: e.tensor_scalar(out=msa, in0=ssa, scalar1=1.0 / 128, scalar2=EPS,
                                                      op0=ALU.mult, op1=ALU.add),
                     r=[("ssa", q) for q in range(4)], w=["msa"])
                P.op("pool", lambda e: e.tensor_tensor(out=rsa, in0=msa, in1=neghalf[:, 0:4], op=ALU.pow),
                     r=["msa", "neghalf"], w=["rsa"])
                for qs in range(4):
                    P.op("dve", lambda e, qs=qs: e.scalar_tensor_tensor(
                        out=an[:, qs, :], in0=aB[:, qs, :], scalar=rsa[:, qs:qs + 1], in1=sgain08[:],
                        op0=ALU.mult, op1=ALU.mult),
                         r=[("aB", qs), "rsa", "sgain08"], w=[("an", qs)])

            def epiB(h):
                for qs in range(4):
                    P.op("pe", lambda e, qs=qs: e.transpose(out=ps3b[:, qs * 128:(qs + 1) * 128], in_=an[:, qs, :],
                                                            identity=identb[:]),
                         r=[("an", qs), "identb"], w=[kps(3)])
                P.op("act", lambda e: e.activation(out=QT[:, h, 0, :], in_=ps3b[:, 0:512], func=AF.Copy),
                     r=[kps(3)], w=[kQ(h, 0)])

            pending = [None]
            for u in range(min(LA, n)):
                qk(u)
            for u in range(n):
                h, j, m = units[u]
                pos = u % per_head
                if pos == 0:
                    zero_O()
                if u + LA < n:
                    qk(u + LA)
                ex(u)
                av(u)
                if pending[0] is not None and pos == min(7, per_head - 1):
                    epiB(pending[0])
                    pending[0] = None
                if pos == per_head - 1:
                    epiA(h)
                    pending[0] = h
            if pending[0] is not None:
                epiB(pending[0])

        def mixer(t):
            norm_to_xnT(1)
            cosb, sinb = R0, R1
            P.dma("sp", "cos", out=cosb, in_=cosT[:, t * T:(t + 1) * T], w=kR0)
            P.dma("sp", "sin", out=sinb, in_=sinT[:, t * T:(t + 1) * T], w=kR1)
            P.op("pool", lambda e: e.memset(QT[64:128, :, 0, :], 0.0), w=[("A", 2 * h) for h in range(4)])
            P.op("pool", lambda e: e.memset(QT[0:64, :, 1, :], 0.0), w=[("A", 2 * h + 1) for h in range(4)])
            uA, uB, uC = ubuf
            for g in range(4):
                wap, wk = need(("U", g))
                for kc in range(8):
                    mm(psb[4 + g][:], wap[:, kc * 128:(kc + 1) * 128], xnT[:, kc, :], kc == 0, kc == 7,
                       r=wk + [("xnT", kc)], w=[kps(4 + g)])
                done()
                P.op("act", lambda e, g=g: e.activation(out=uA[:, 16:528], in_=psb[4 + g][:], func=AF.Copy),
                     r=[kps(4 + g)], w=kub[0])
                P.op("pool", lambda e, g=g: e.tensor_copy(out=uA[:, 0:16], in_=halo[:, g, :]),
                     r=[("halo", g), "halo"], w=kub[0])
                P.op("pool", lambda e, g=g: e.tensor_copy(out=halo[:, g, :], in_=uA[:, 512:528]),
                     r=kub[0], w=[("halo", g)])
                cur, kcur = uA, kub[0]
                for lvl in range(g + 1):
                    sh = 1 << lvl
                    lo = 2 * sh - 1
                    dst, kdst = (uB, kub[1]) if lvl % 2 == 0 else (uC, kub[2])
                    P.op("pool", lambda e, cur=cur, dst=dst, lo=lo, sh=sh: e.tensor_tensor(
                        out=dst[:, lo:528], in0=cur[:, lo:528], in1=cur[:, lo - sh:528 - sh], op=ALU.add),
                         r=kcur, w=kdst)
                    cur, kcur = dst, kdst
                win = 2 << g
                P.op("dve", lambda e, g=g, cur=cur, win=win: e.scalar_tensor_tensor(
                    out=dT[:, g, :], in0=cur[:, 16:528], scalar=1.0 / win, in1=uA[:, 16:528],
                    op0=ALU.mult, op1=ALU.subtract), r=kcur + kub[0], w=[("A", 12 + g)])
                if t == 0:
                    n = win - 1
                    P.op("dve", lambda e, cur=cur, n=n: e.tensor_tensor(out=tmpf[:, 0:n], in0=cur[:, 16:16 + n],
                                                                        in1=rcnt[:, 0:n], op=ALU.mult),
                         r=kcur + ["rcnt"], w=["tmpf"])
                    P.op("dve", lambda e, g=g, n=n: e.tensor_tensor(out=dT[:, g, 0:n], in0=tmpf[:, 0:n],
                                                                    in1=uA[:, 16:16 + n], op=ALU.subtract),
                         r=["tmpf"] + kub[0], w=[("A", 12 + g)])
            vsl = [need(("V", v)) for v in range(4)]
            for v in range(4):
                for kk in range(2):
                    kc = 2 * v + kk
                    for s in range(4):
                        mm(psb[s][:], xnT[:, kc, s * 128:(s + 1) * 128], vsl[v][0][:, kk * 512:(kk + 1) * 512],
                           kc == 0, kc == 7, r=vsl[v][1] + [("xnT", kc)], w=[kps(s)])
                done()
            for s in range(4):
                P.op("act", lambda e, s=s: e.activation(out=VC[:, 4 * t + s, :, 0:128],
                                                        in_=psb[s][:].rearrange("p (h d) -> p h d", h=4),
                                                        func=AF.Copy),
                     r=[kps(s)], w=[("VC", 4 * t + s)])
            for h in range(4):
                bo = 4 if h % 2 == 0 else 0
                for (nm, bank) in (("Q", bo), ("QS", bo + 1), ("K", bo + 2), ("KS", bo + 3)):
                    wap, wk = need((nm, h))
                    for kc in range(8):
                        mm(psb[bank][:], wap[:, kc * 128:(kc + 1) * 128], xnT[:, kc, :], kc == 0, kc == 7,
                           r=wk + [("xnT", kc)], w=[kps(bank)])
                    done()
                for (b0, isq) in ((bo, True), (bo + 2, False)):
                    P.op("dve", lambda e, b0=b0: e.tensor_tensor(out=psb[b0][:], in0=psb[b0][:], in1=cosb,
                                                                 op=ALU.mult), r=[kps(b0)] + kR0, w=[kps(b0)])
                    P.op("dve", lambda e, b0=b0: e.tensor_tensor(out=rstd_b[:], in0=psb[b0 + 1][:], in1=sinb,
                                                                 op=ALU.mult), r=[kps(b0 + 1)] + kR1, w=["rstd_b"])
                    if isq:
                        P.op("dve", lambda e, h=h, b0=b0: e.tensor_tensor(out=QT[0:64, h, 0, :], in0=psb[b0][0:64, :],
                                                                          in1=rstd_b[0:64, :], op=ALU.add),
                             r=[kps(b0), "rstd_b"], w=[("A", 2 * h)])
                        P.op("dve", lambda e, h=h, b0=b0: e.tensor_tensor(out=QT[64:128, h, 1, :],
                                                                          in0=psb[b0][64:128, :],
                                                                          in1=rstd_b[64:128, :], op=ALU.add),
                             r=[kps(b0), "rstd_b"], w=[("A", 2 * h + 1)])
                    else:
                        P.op("dve", lambda e, h=h, b0=b0: e.tensor_tensor(out=KT[:, h, t * T:(t + 1) * T],
                                                                          in0=psb[b0][:], in1=rstd_b[:], op=ALU.add),
                             r=[kps(b0), "rstd_b"], w=[("KT", h, t)])
            attention_tile(t)
            for g in range(4):
                bank = 4 + g % 2
                mm(psb[bank][:], pwb[:, g * 128:(g + 1) * 128], dT[:, g, :], True, True,
                   r=["pwb", ("A", 12 + g)], w=[kps(bank)])
                P.op("act", lambda e, g=g, bank=bank: e.activation(out=poolT[:, g, :], in_=psb[bank][:], func=AF.Copy,
                                                                   scale=pscale[:, g:g + 1]),
                     r=[kps(bank), "pscale"], w=[("A", 8 + g)])
            if dbg:
                P.dma("sp", "dbg", out=dbg_t["catT"][t][:, 0:4, :], in_=QT[:, :, 0, :], r=[("A", i) for i in range(8)])
                P.dma("sp", "dbg", out=dbg_t["catT"][t][:, 4:8, :], in_=poolT[:, :, :], r=[("A", 8 + i) for i in range(4)])
                P.dma("sp", "dbg", out=dbg_t["dT"][t], in_=dT[:, :, :], r=[("A", 12 + i) for i in range(4)])
            for kc in range(8):
                wap, wk = need(("WO", kc))
                lhs = QT[:, kc, 0, :] if kc < 4 else poolT[:, kc - 4, :]
                kl = ("A", 2 * kc) if kc < 4 else ("A", 8 + kc - 4)
                for s in range(4):
                    for hf in range(2):
                        b = s * 2 + hf
                        mm(psb[b][:], lhs[:, s * 128:(s + 1) * 128], wap[:, hf * 512:(hf + 1) * 512],
                           kc == 0, kc == 7, r=wk + [kl], w=[kps(b)])
                done()
            for s in range(4):
                for hf in range(2):
                    b = s * 2 + hf
                    xs = xres[:, s, hf * 512:(hf + 1) * 512]
                    P.op("dve", lambda e, b=b, xs=xs: e.tensor_tensor(out=xs, in0=psb[b][:], in1=xs, op=ALU.add),
                         r=[kps(b), ("xres", s)], w=[("xres", s)])

        last_store = [None]

        ssqF = small[:, 52:56]
        mseF = small[:, 56:60]
        rstdF = small[:, 60:64]

        def final(t):
            P.dma("sp", "gfin", out=gfinb, in_=gfin_d,
                  w=kR0 + kR1 + [("aA", 0), ("aA", 1)] + [("aB", q) for q in range(4)] + [("an", q) for q in range(4)])
            order = (3, 2, 1, 0)
            for s in order:
                P.op("act", lambda e, s=s: e.activation(out=junk_sb, in_=stage[:, s, :], func=AF.Square,
                                                        accum_out=ssqF[:, s:s + 1]),
                     r=blk(s * 4 * KB, 4 * KB), w=kjunk + [("ssqF", s)])
                P.op("dve", lambda e, s=s: e.tensor_scalar(out=mseF[:, s:s + 1], in0=ssqF[:, s:s + 1],
                                                           scalar1=1.0 / D, scalar2=EPS, op0=ALU.mult, op1=ALU.add),
                     r=[("ssqF", s)], w=[("mseF", s)])
                P.op("pool", lambda e, s=s: e.tensor_tensor(out=rstdF[:, s:s + 1], in0=mseF[:, s:s + 1],
                                                            in1=neghalf[:, 0:1], op=ALU.pow),
                     r=[("mseF", s), "neghalf"], w=[("rstdF", s)])
            stores = []
            for s in order:
                kb = blk(s * 4 * KB, 4 * KB)
                P.op("act", lambda e, s=s: e.activation(out=stage[:, s, :], in_=stage[:, s, :], func=AF.Copy,
                                                        scale=rstdF[:, s:s + 1]),
                     r=[("rstdF", s)] + kb, w=kb)
                P.op("pool", lambda e, s=s: e.tensor_tensor(out=stage[:, s, :], in0=stage[:, s, :], in1=gfinb,
                                                            op=ALU.mult),
                     r=kR0 + kR1 + kb, w=kb)
                stores.append(P.dma("sp", f"ost{s}", out=out[t * T + s * 128:t * T + (s + 1) * 128, :],
                                    in_=stage[:, s, :], r=kb))
            last_store[0] = stores

        def load_x(t):
            for s in range(4):
                P.dma("sp", f"x{s}", out=xres[:, s, :], in_=x[t * T + s * 128:t * T + (s + 1) * 128, :],
                      w=[("xres", s)])

        load_x(0)
        norm_to_xnT(0)
        for t in range(NT):
            ffn(1, 0, pre_normed=True)
            mixer(t)
            ffn(2, 2, out_stage=True)
            if t + 1 < NT:
                load_x(t + 1)
                norm_to_xnT(0)
            final(t)
        P.op("sp", lambda e: e.nop(), extra=last_store[0])
        P.op("act", lambda e: e.nop(), extra=last_store[0])

        nwaits = P.finalize()
        sem_names = set(P.ENG) | set(P.dcnt.keys())
        sems = {n: es.enter_context(nc.semaphore(n)) for n in sorted(sem_names)}

        def replay(name):
            def run(e):
                for it in P.q[name]:
                    for (sn, v) in it.waits:
                        e.wait_ge(sems[sn], v)
                    ins = it.fn(e)
                    if it.kind == "dma":
                        ins.then_inc(sems[it.dsem], 16)
                    elif it.marked:
                        ins.then_inc(sems[it.eng], 1)
            return run

        with nc.Block() as block:
            block.tensor(replay("pe"))
            block.scalar(replay("act"))
            block.vector(replay("dve"))
            block.gpsimd(replay("pool"))
            block.sync(replay("sp"))
    nc._stats = {e: len(P.q[e]) for e in P.ENG}
    nc._stats["waits"] = nwaits
    return nc


def host_consts(S):
    inv_freq = (1.0 / (np.float32(10000.0) ** (np.arange(0, 64, 2, dtype=np.float32) / np.float32(64)))).astype(np.float32)
    pos = np.arange(S, dtype=np.float32)
    ang = (pos[:, None] * inv_freq[None, :]).astype(np.float32)
    c = np.cos(ang).astype(np.float32).T
    s = np.sin(ang).astype(np.float32).T
    cos64 = np.concatenate([c, c], axis=0)
    sin64 = np.concatenate([-s, s], axis=0)
    cosT = np.ascontiguousarray(np.concatenate([cos64, cos64], axis=0))
    sinT = np.ascontiguousarray(np.concatenate([sin64, sin64], axis=0))
    p = np.arange(128)[:, None]
    cc = np.arange(896)[None, :]
    maskm = np.where(cc >= 384 + 64 * (p >= 64), 0.0, -30000.0).astype(ml_dtypes.bfloat16)
    rcnt = np.broadcast_to((1.0 / np.arange(1, 17, dtype=np.float32))[None, :], (128, 16)).copy()
    return dict(cosT=cosT, sinT=sinT, ident=np.eye(128, dtype=np.float32), maskm=maskm, rcnt=rcnt)


def host_params(inp):
    f32 = lambda a: np.ascontiguousarray(np.asarray(a, dtype=np.float32))
    g3 = [f32(inp[k])[0].reshape(8, 128).T for k in ("ffn1_norm", "mix_norm", "ffn2_norm")]
    m = dict(
        gains=np.ascontiguousarray(np.concatenate(g3, axis=1)),
        gfin=np.ascontiguousarray(np.broadcast_to(f32(inp["final_norm"])[None, :], (128, D))),
        lamv=np.ascontiguousarray(np.broadcast_to(np.concatenate(
            [f32(inp[k])[0] for k in ("lambda_q1", "lambda_k1", "lambda_q2", "lambda_k2")])[None, :], (128, 256))),
        sgain=np.ascontiguousarray(np.broadcast_to(f32(inp["subln_gain"])[0][None, :], (128, 128))),
        pscale=np.ascontiguousarray(f32(inp["pool_scale"])[0].reshape(4, 128).T),
        w_in=f32(inp["w_in"])[0], w_out=f32(inp["w_out"])[0], pool_w=f32(inp["pool_w"])[0],
    )
    for f in (1, 2):
        for nm in ("gate", "up", "down"):
            m[f"ffn{f}_w_{nm}"] = f32(inp[f"ffn{f}_w_{nm}"])[0]
    return m


def kernel(**inputs):
    x = np.asarray(inputs["x"], dtype=np.float32)
    B, S, _ = x.shape
    nc = build(S)
    shared = host_params(inputs)
    shared.update(host_consts(S))
    in_maps = []
    for b in range(B):
        m = dict(shared)
        m["x"] = np.ascontiguousarray(x[b])
        in_maps.append(m)
    res = run_bass_kernel_spmd(nc, in_maps, core_ids=list(range(B)))
    return np.stack([np.asarray(r["out"], dtype=np.float32) for r in res.results], axis=0)
```
